# Optimizing a Trainium2 kernel written in Bass

```python
import jax, jax.numpy as jnp
from jax import lax
import numpy as np

D_MODEL = 1024
BATCH = 8
SEQ = 2048
DEPTH = 2
DEC_BATCH = 128
DEC_SEQ = 8
PAST_LEN = 16384
PAGE_SIZE = 128

SSD_HEAD_DIM = 64
SSD_INNER = D_MODEL
SSD_HEADS = SSD_INNER // SSD_HEAD_DIM
SSD_GROUPS = 4
SSD_STATE = 128
SSD_CONV = 4
SSD_CHUNK = 128
D_XBC = SSD_INNER + 2 * SSD_GROUPS * SSD_STATE
D_SC = D_MODEL
SC_CONV = 3
D_FF = 2816
N_MOD = 9
EPS = 1e-6
D_PROJ = SSD_INNER + D_XBC + SSD_HEADS + 3 * D_SC + 2 * D_MODEL

kernel_name = "hybrid_ssd_shortconv_adaln_decoder_step"


def _split(t, sizes):
    idx = np.cumsum(sizes)[:-1].tolist()
    return jnp.split(t, idx, axis=-1)


def rmsnorm(x, g):
    xf = x.astype(jnp.float32)
    r = lax.rsqrt(jnp.mean(xf * xf, axis=-1, keepdims=True) + EPS)
    return (xf * r).astype(x.dtype) * g


def group_rmsnorm(x, g, n_groups):
    shp = x.shape
    xf = x.astype(jnp.float32).reshape(shp[:-1] + (n_groups, shp[-1] // n_groups))
    r = lax.rsqrt(jnp.mean(xf * xf, axis=-1, keepdims=True) + EPS)
    return (xf * r).reshape(shp).astype(x.dtype) * g


def modulate(h, shift, scale):
    return h * (1.0 + scale[:, None, :]) + shift[:, None, :]


def causal_dwconv(u, buf, w):
    k_w = w.shape[0]
    L = u.shape[1]
    ext = jnp.concatenate([buf.astype(u.dtype), u], axis=1)
    y = ext[:, 0:L] * w[0]
    for k in range(1, k_w):
        y = y + ext[:, k:k + L] * w[k]
    return y, ext[:, L:]


def ssd_scan(xh, dt, a, bm, cm, s0):
    bsz, L, H, P = xh.shape
    G, N = bm.shape[2], bm.shape[3]
    R = H // G
    T = SSD_CHUNK if L >= SSD_CHUNK else L
    pad = (-L) % T
    f32 = jnp.float32
    xh, dt, bm, cm = xh.astype(f32), dt.astype(f32), bm.astype(f32), cm.astype(f32)
    if pad:
        pw = ((0, 0), (0, pad), (0, 0))
        xh = jnp.pad(xh, pw + ((0, 0),))
        dt = jnp.pad(dt, pw)
        bm = jnp.pad(bm, pw + ((0, 0),))
        cm = jnp.pad(cm, pw + ((0, 0),))
    nc = (L + pad) // T
    x = xh.reshape(bsz, nc, T, G, R, P)
    dtc = dt.reshape(bsz, nc, T, G, R)
    b_ = bm.reshape(bsz, nc, T, G, N)
    c_ = cm.reshape(bsz, nc, T, G, N)
    xdt = x * dtc[..., None]
    da = jnp.transpose(dtc * a.astype(f32).reshape(G, R), (0, 1, 3, 4, 2))
    acum = jnp.cumsum(da, axis=-1)
    seg = acum[..., :, None] - acum[..., None, :]
    mask = jnp.tril(jnp.ones((T, T), dtype=bool))
    decay = jnp.exp(jnp.where(mask, seg, -jnp.inf))
    cb = jnp.einsum('bclgn,bcsgn->bcgls', c_, b_)
    y_diag = jnp.einsum('bcgls,bcgrls,bcsgrp->bclgrp', cb, decay, xdt)
    decay_states = jnp.exp(acum[..., -1:] - acum)
    states = jnp.einsum('bclgn,bcgrl,bclgrp->bcgrpn', b_, decay_states, xdt)
    chunk_decay = jnp.exp(acum[..., -1])

    def step(s, inp):
        st, dec = inp
        return dec[..., None, None] * s + st, s

    s_final, s_in = lax.scan(step, s0.astype(f32).reshape(bsz, G, R, P, N),
                             (jnp.moveaxis(states, 1, 0), jnp.moveaxis(chunk_decay, 1, 0)))
    s_in = jnp.moveaxis(s_in, 0, 1)
    y_off = jnp.einsum('bclgn,bcgrpn,bcgrl->bclgrp', c_, s_in, jnp.exp(acum))
    y = (y_diag + y_off).reshape(bsz, nc * T, H, P)[:, :L]
    return y, s_final.reshape(bsz, H, P, N)


def swiglu(h, w_gu, w_down):
    g, u = jnp.split(h @ w_gu, 2, axis=-1)
    return (jax.nn.silu(g) * u) @ w_down


def trunk_layer(x, c, s_ssm, buf_xbc, buf_sc,
                w_ada, b_ada, norm_ffn1, norm_mix, norm_ffn2,
                ffn1_w_gu, ffn1_w_down, ffn2_w_gu, ffn2_w_down,
                w_in, ssd_conv_w, ssd_conv_b, ssd_dt_bias, ssd_a_log, ssd_d, ssd_norm,
                w_out_ssd, sc_conv_w, w_out_sc, w_o):
    bsz, L, _ = x.shape
    mod = jax.nn.silu(c) @ w_ada + b_ada
    sh1, sc1, g1, sh2, sc2, g2, sh3, sc3, g3 = jnp.split(mod, N_MOD, axis=-1)

    h = modulate(rmsnorm(x, norm_ffn1), sh1, sc1)
    x = x + 0.5 * g1[:, None, :] * swiglu(h, ffn1_w_gu, ffn1_w_down)

    h = modulate(rmsnorm(x, norm_mix), sh2, sc2)
    proj = h @ w_in
    z, xbc, dt_raw, sc_b, sc_c, sc_h, gate_ssd, gate_sc = _split(
        proj, [SSD_INNER, D_XBC, SSD_HEADS, D_SC, D_SC, D_SC, D_MODEL, D_MODEL])

    xbc_c, new_buf_xbc = causal_dwconv(xbc, buf_xbc, ssd_conv_w)
    xbc_c = jax.nn.silu(xbc_c + ssd_conv_b)
    xs, bs, cs = _split(xbc_c, [SSD_INNER, SSD_GROUPS * SSD_STATE, SSD_GROUPS * SSD_STATE])
    dt = jax.nn.softplus(dt_raw.astype(jnp.float32) + ssd_dt_bias.astype(jnp.float32))
    a = -jnp.exp(ssd_a_log.astype(jnp.float32))
    xh = xs.reshape(bsz, L, SSD_HEADS, SSD_HEAD_DIM)
    y, s_new = ssd_scan(xh, dt, a,
                        bs.reshape(bsz, L, SSD_GROUPS, SSD_STATE),
                        cs.reshape(bsz, L, SSD_GROUPS, SSD_STATE), s_ssm)
    y = y.astype(x.dtype) + xh * ssd_d[:, None]
    y = y.reshape(bsz, L, SSD_INNER) * jax.nn.silu(z)
    y = group_rmsnorm(y, ssd_norm, SSD_GROUPS)
    y_ssd = y @ w_out_ssd

    u, new_buf_sc = causal_dwconv(sc_c * sc_h, buf_sc, sc_conv_w)
    y_sc = (sc_b * u) @ w_out_sc

    merged = jax.nn.sigmoid(gate_ssd) * y_ssd + jax.nn.sigmoid(gate_sc) * y_sc
    x = x + g2[:, None, :] * (merged @ w_o)

    h = modulate(rmsnorm(x, norm_ffn2), sh3, sc3)
    x = x + 0.5 * g3[:, None, :] * swiglu(h, ffn2_w_gu, ffn2_w_down)
    return x, s_new.astype(s_ssm.dtype), new_buf_xbc, new_buf_sc


def setup_inputs(seed: int = 0) -> dict:
    key = jax.random.key(seed)
    ks = iter(jax.random.split(key, 40))
    nrm = lambda shape, s: jax.random.normal(next(ks), shape, jnp.float32) * s
    gain = lambda shape: 1.0 + nrm(shape, 0.02)
    dt0 = jnp.exp(jax.random.uniform(next(ks), (DEPTH, SSD_HEADS), jnp.float32,
                                     np.log(1e-3), np.log(1e-1)))
    a0 = jax.random.uniform(next(ks), (DEPTH, SSD_HEADS), jnp.float32, 1.0, 16.0)
    return {
        "x_prompt": nrm((BATCH, SEQ, D_MODEL), 1.0),
        "x_sample": nrm((DEC_BATCH, DEC_SEQ, D_MODEL), 1.0),
        "c_prompt": nrm((BATCH, D_MODEL), 1.0),
        "c_sample": nrm((DEC_BATCH, D_MODEL), 1.0),
        "state_ssm": nrm((DEPTH, DEC_BATCH, SSD_HEADS, SSD_HEAD_DIM, SSD_STATE), 0.3),
        "state_conv_ssd": nrm((DEPTH, DEC_BATCH, SSD_CONV - 1, D_XBC), 1.0),
        "state_conv_short": nrm((DEPTH, DEC_BATCH, SC_CONV - 1, D_SC), 1.0),
        "w_ada": nrm((DEPTH, D_MODEL, N_MOD * D_MODEL), 0.5 * D_MODEL ** -0.5),
        "b_ada": nrm((DEPTH, N_MOD * D_MODEL), 0.02),
        "norm_ffn1": gain((DEPTH, D_MODEL)),
        "norm_mix": gain((DEPTH, D_MODEL)),
        "norm_ffn2": gain((DEPTH, D_MODEL)),
        "ffn1_w_gu": nrm((DEPTH, D_MODEL, 2 * D_FF), D_MODEL ** -0.5),
        "ffn1_w_down": nrm((DEPTH, D_FF, D_MODEL), D_FF ** -0.5),
        "ffn2_w_gu": nrm((DEPTH, D_MODEL, 2 * D_FF), D_MODEL ** -0.5),
        "ffn2_w_down": nrm((DEPTH, D_FF, D_MODEL), D_FF ** -0.5),
        "w_in": nrm((DEPTH, D_MODEL, D_PROJ), D_MODEL ** -0.5),
        "ssd_conv_w": nrm((DEPTH, SSD_CONV, D_XBC), SSD_CONV ** -0.5),
        "ssd_conv_b": nrm((DEPTH, D_XBC), 0.02),
        "ssd_dt_bias": dt0 + jnp.log(-jnp.expm1(-dt0)),
        "ssd_a_log": jnp.log(a0),
        "ssd_d": gain((DEPTH, SSD_HEADS)),
        "ssd_norm": gain((DEPTH, SSD_INNER)),
        "w_out_ssd": nrm((DEPTH, SSD_INNER, D_MODEL), SSD_INNER ** -0.5),
        "sc_conv_w": nrm((DEPTH, SC_CONV, D_SC), SC_CONV ** -0.5),
        "w_out_sc": nrm((DEPTH, D_SC, D_MODEL), D_SC ** -0.5),
        "w_o": nrm((DEPTH, D_MODEL, D_MODEL), D_MODEL ** -0.5),
        "norm_final": gain((D_MODEL,)),
    }


def reference(x_prompt, x_sample, c_prompt, c_sample, state_ssm, state_conv_ssd,
              state_conv_short, w_ada, b_ada, norm_ffn1, norm_mix, norm_ffn2,
              ffn1_w_gu, ffn1_w_down, ffn2_w_gu, ffn2_w_down, w_in, ssd_conv_w,
              ssd_conv_b, ssd_dt_bias, ssd_a_log, ssd_d, ssd_norm, w_out_ssd,
              sc_conv_w, w_out_sc, w_o, norm_final):
    bp = x_prompt.shape[0]
    dtp = x_prompt.dtype
    xp, xs = x_prompt, x_sample
    ssm_p, cx_p, cs_p, ssm_s, cx_s, cs_s = [], [], [], [], [], []
    for l in range(DEPTH):
        lw = (w_ada[l], b_ada[l], norm_ffn1[l], norm_mix[l], norm_ffn2[l],
              ffn1_w_gu[l], ffn1_w_down[l], ffn2_w_gu[l], ffn2_w_down[l],
              w_in[l], ssd_conv_w[l], ssd_conv_b[l], ssd_dt_bias[l], ssd_a_log[l],
              ssd_d[l], ssd_norm[l], w_out_ssd[l], sc_conv_w[l], w_out_sc[l], w_o[l])
        xp, s1, b1, b2 = trunk_layer(
            xp, c_prompt,
            jnp.zeros((bp, SSD_HEADS, SSD_HEAD_DIM, SSD_STATE), dtp),
            jnp.zeros((bp, SSD_CONV - 1, D_XBC), dtp),
            jnp.zeros((bp, SC_CONV - 1, D_SC), dtp), *lw)
        ssm_p.append(s1); cx_p.append(b1); cs_p.append(b2)
        xs, s1, b1, b2 = trunk_layer(
            xs, c_sample, state_ssm[l], state_conv_ssd[l], state_conv_short[l], *lw)
        ssm_s.append(s1); cx_s.append(b1); cs_s.append(b2)
    y_prompt = rmsnorm(xp, norm_final)
    y_sample = rmsnorm(xs, norm_final)
    return (y_prompt, y_sample,
            jnp.stack(ssm_p), jnp.stack(cx_p), jnp.stack(cs_p),
            jnp.stack(ssm_s), jnp.stack(cx_s), jnp.stack(cs_s))
```

```python
import numpy as np
from contextlib import ExitStack
import concourse.bass as bass
import concourse.mybir as mybir
from concourse.bass_utils import run_bass_kernel_spmd

F32 = mybir.dt.float32
BF16 = mybir.dt.bfloat16
AF = mybir.ActivationFunctionType
ALU = mybir.AluOpType

NCORES = 8
D = 1024
DFF = 2816
NJ = DFF // 128
LP = 2048
NS = 16
LS = 8
NTOK = LP + NS * LS
DPROJ = 8208
EPS = 1e-6
WB = 256
NSLOT = 5
C_Z, C_XBC, C_SCB, C_SCC, C_SCH, C_GSSD, C_GSC = 0, 1024, 3072, 4096, 5120, 6144, 7168
C_DT_RAW = 3072

TILES = [(0, 512, False), (512, 512, False), (1024, 512, False), (1536, 512, False), (2048, 128, True)]
SUPER = [[0, 1, 4], [2, 3]]

def V_NF1(l): return 0 + l
def V_NMIX(l): return 2 + l
def V_NF2(l): return 4 + l
def V_SSDN(l): return 6 + l
def V_DP(l): return 8 + l
def V_SCW(l, k): return 10 + 3 * l + k
V_NFIN = 16
NV = 17
def X_CW(l, k): return 5 * l + k
def X_CB(l): return 5 * l + 4
NX = 10


class Sem:
    __slots__ = ("h", "val", "name")

    def __init__(self, h, name):
        self.h = h
        self.val = 0
        self.name = name


class Res:
    __slots__ = ("name", "w", "r", "excl")

    def __init__(self, name, excl=False):
        self.name = name
        self.w = None
        self.r = {}
        self.excl = excl


class Prog:
    ENGS = ("pe", "act", "dve", "pool", "sp")

    def __init__(self, nc, stack):
        self.nc = nc
        self.stack = stack
        self.q = {e: [] for e in self.ENGS}
        self.esem = {e: self.new_sem("e_" + e) for e in self.ENGS}
        self.waited = {e: {} for e in self.ENGS}
        self.nops = 0

    def new_sem(self, name):
        h = self.stack.enter_context(self.nc.semaphore(name))
        return Sem(h, name)

    def op(self, eng, fn, reads=(), writes=(), inc=True, dma_sem=None):
        need = {}
        ex = [r for r in reads if r.excl]
        if ex:
            writes = list(writes) + ex
            reads = [r for r in reads if not r.excl]

        def add(st):
            if st is None:
                return
            s, v = st
            if need.get(s, 0) < v:
                need[s] = v

        for r in reads:
            add(r.w)
        for w in writes:
            add(w.w)
            for s, v in w.r.items():
                add((s, v))
        mysem = self.esem[eng]
        wd = self.waited[eng]
        for s, v in need.items():
            if s is mysem:
                if eng == "pe" or v > s.val:
                    continue
            if wd.get(s, 0) >= v:
                continue
            wd[s] = v
            self.q[eng].append(("w", s, v))
        if dma_sem is not None:
            dma_sem.val += 16
            st = (dma_sem, dma_sem.val)
            self.q[eng].append(("o", fn, dma_sem, 16))
        elif inc:
            mysem.val += 1
            st = (mysem, mysem.val)
            self.q[eng].append(("o", fn, mysem, 1))
        else:
            st = (mysem, mysem.val + 1)
            self.q[eng].append(("o", fn, None, 0))
        for r in reads:
            s, v = st
            if r.r.get(s, 0) < v:
                r.r[s] = v
        for w in writes:
            w.w = st
            w.r = {}
        self.nops += 1
        return st

    def replay(self, eng, e):
        for it in self.q[eng]:
            if it[0] == "w":
                e.wait_ge(it[1].h, it[2])
            else:
                ins = it[1](e)
                if it[2] is not None:
                    ins.then_inc(it[2].h, it[3])


def handoff(src, dst):
    allst = {}
    for r in src:
        if r.w is not None:
            s, v = r.w
            if allst.get(s, 0) < v:
                allst[s] = v
        for s, v in r.r.items():
            if allst.get(s, 0) < v:
                allst[s] = v
    for d in dst:
        for s, v in allst.items():
            if d.r.get(s, 0) < v:
                d.r[s] = v


class Arena:
    def __init__(self, nc, base, end):
        self.nc = nc
        self.off = base
        self.end = end
        self.n = 0

    def alloc(self, name, shape, dtype, at=None):
        esz = 4 if dtype == F32 else 2
        nbytes = esz
        for s in shape[1:]:
            nbytes *= s
        if at is None:
            off = (self.off + 63) // 64 * 64
            self.off = off + nbytes
            assert self.off <= self.end, f"SBUF arena overflow at {name}: {self.off} > {self.end}"
        else:
            off = at
            assert off % 32 == 0 and off + nbytes <= self.end, name
        self.n += 1
        return self.nc.alloc_sbuf_tensor_at(f"{name}_{self.n}", list(shape), dtype, offset=off)


def build_program(stop_after=None, dbg=False):
    nc = bass.Bass("TRN2", target_bir_lowering=False)
    stack = ExitStack()
    P = Prog(nc, stack)
    A = Arena(nc, 16896, 229376)

    def din(name, shape):
        return nc.dram_tensor(name, list(shape), F32, kind="ExternalInput").ap()

    def dout(name, shape):
        return nc.dram_tensor(name, list(shape), F32, kind="ExternalOutput").ap()

    xT_in = din("xT_in", [D, NTOK])
    cT_in = din("cT_in", [128, 8, 17])
    s0A_in = din("s0A_in", [2, NS, 1024, 128])
    s0T_in = din("s0T_in", [2, NS, 128, 1024])
    csx_in = din("csx_in", [2, 128, 16, NS, 3])
    css_in = din("css_in", [2, 128, 8, NS, 2])
    w_ada = din("w_ada", [2, 9 * D // WB, 128, 8 * WB])
    badaT_in = din("badaT_in", [128, 2, 72])
    vecsT_in = din("vecsT_in", [128, 8, NV])
    xbcv_in = din("xbcv_in", [128, 16, NX])
    dtb_in = din("dtb_in", [16, 2])
    alog_in = din("alog_in", [128, 32])
    consts_in = din("consts_in", [128, 10, 128])
    w_gu = [din("ffn1_w_gu", [2, 2 * DFF // WB, 128, 8 * WB]), din("ffn2_w_gu", [2, 2 * DFF // WB, 128, 8 * WB])]
    w_dn = [din("ffn1_w_down", [2, 8, 128, NJ * 128]), din("ffn2_w_down", [2, 8, 128, NJ * 128])]
    w_in = din("w_in", [2, (DPROJ - 16) // WB, 128, 8 * WB])
    w_dt = din("w_dt", [2, 128, 128])
    w_out_ssd = din("w_out_ssd", [2, D // WB, 128, 8 * WB])
    w_out_sc = din("w_out_sc", [2, D // WB, 128, 8 * WB])
    w_o = din("w_o", [2, D // WB, 128, 8 * WB])

    yT_out = dout("yT_out", [D, NTOK])
    ssm_p_out = dout("ssm_p_out", [2, 1024, 128])
    csx_p_out = dout("csx_p_out", [2, 3, 2048])
    css_p_out = dout("css_p_out", [2, 2, 1024])
    ssm_s_out = dout("ssm_s_out", [2, NS, 1024, 128])
    csx_s_out = dout("csx_s_out", [2, NS, 3, 2048])
    css_s_out = dout("css_s_out", [2, NS, 2, 1024])

    psum = nc.alloc_psum_tensor("psum", [128, 4096], F32)
    BANK = [Res(f"bank{b}", excl=True) for b in range(8)]

    def pb(b, n=512, off=0):
        return psum[:, b * 512 + off:b * 512 + off + n]

    def pb16(b, n=1024, off=0):
        return psum[:, b * 512:(b + 1) * 512].bitcast(BF16)[:, off:off + n]

    xT = A.alloc("xT", [128, 8, NTOK], F32)
    XR = [Res(f"x{k}") for k in range(5)]
    wslot = [A.alloc(f"ws{i}", [128, 8 * WB], BF16) for i in range(NSLOT)]
    WR = [Res(f"ws{i}") for i in range(NSLOT)]
    wsem = [P.new_sem(f"wsem{i}") for i in range(NSLOT)]
    wctr = [0]
    consts = A.alloc("consts", [128, 10, 128], F32)
    R_const = Res("consts")
    ident_f = consts[:, 0, :]
    ones_f = consts[:, 1, :]
    U_f = [consts[:, 2, :], consts[:, 5, :]]
    MB_f = [consts[:, 3, :], consts[:, 6, :]]
    M01_f = [consts[:, 4, :], consts[:, 7, :]]
    ind_f = consts[:, 8, 0:16]
    sel_f = consts[:, 9, :]
    ident_b = A.alloc("ident_b", [128, 128], BF16)
    ones_b = A.alloc("ones_b", [128, 128], BF16)
    mhalf = A.alloc("mhalf", [128, 1], F32)
    R_cb = Res("const_b")
    vecsT = A.alloc("vecsT", [128, 8, NV], F32)
    xbcv = A.alloc("xbcv", [128, 16, NX], F32)
    dtb = A.alloc("dtb", [16, 2], F32)
    a_bc = A.alloc("a_bc", [128, 32], F32)
    cT = A.alloc("cT", [128, 8, 17], F32)
    scT = A.alloc("scT", [128, 8, 17], BF16)
    badaT = A.alloc("badaT", [128, 2, 72], F32)
    R_tab = Res("tables")
    R_abc = Res("a_bc")
    R_scT = Res("scT")
    modT = [A.alloc(f"modT{l}", [128, 72, 17], F32) for l in range(2)]
    R_modt = [[Res(f"mod{l}_{t}") for t in range(9)] for l in range(2)]
    hist_x = A.alloc("hist_x", [128, 16, 3], F32)
    hist_s = A.alloc("hist_s", [128, 8, 2], F32)
    R_hx = Res("hist_x")
    R_hs = Res("hist_s")
    n_sq = [A.alloc(f"n_sq{i}", [128, 512], BF16) for i in range(2)]
    R_nsq = [Res("nsq0"), Res("nsq1")]
    n_tmp = [A.alloc(f"n_tmp{i}", [128, 512], F32) for i in range(2)]
    R_ntmp = [Res("ntmp0"), Res("ntmp1")]
    n_rstd = A.alloc("n_rstd", [128, 512], F32)
    R_nv = Res("n_v")
    R_nrstd = Res("n_rstd")

    OV = A.off
    OV = (OV + 63) // 64 * 64

    A.off = OV
    hT_st = A.alloc("hT_st", [128, 8, 1152], BF16)
    hid = A.alloc("hid", [128, NJ, 1152], BF16)
    wdq = [A.alloc(f"wdq{i}", [128, NJ, 128], BF16) for i in range(3)]
    f_sg = [A.alloc(f"f_sg{i}", [128, 512], F32) for i in range(2)]
    FFN_END = A.off
    R_hTst = [Res(f"hTst{i}") for i in range(3)]
    R_hid = [Res(f"hid{i}") for i in range(3)]
    R_wdq = [Res("wdq0"), Res("wdq1"), Res("wdq2")]
    wdsem = [P.new_sem("wdsem0"), P.new_sem("wdsem1"), P.new_sem("wdsem2")]
    R_fsg = [Res("fsg0"), Res("fsg1")]
    FFN_RES = R_hTst + R_hid + R_wdq + R_fsg

    A.off = OV
    hT = A.alloc("hT", [128, 8, 512], BF16)
    xbc_c = A.alloc("xbc_c", [128, 16, 512], BF16)
    sz = A.alloc("sz", [128, 8, 512], BF16)
    ynT = sz
    csout = ynT
    dtT = A.alloc("dtT", [16, 512], F32)
    R_hT = Res("hT")
    R_xc = [Res(f"xc{m}") for m in range(16)]
    R_sz = Res("sz")
    R_yn = R_sz
    R_dt = Res("dtT")
    sT = A.alloc("sT", [128, 16, 64], F32)
    sT_bf = A.alloc("sT_bf", [128, 16, 64], BF16)
    SSD0 = (A.off + 63) // 64 * 64
    dt_tok = A.alloc("dt_tok", [128, 16], F32)
    da_tok = A.alloc("da_tok", [128, 16], F32)
    nacum = A.alloc("nacum", [128, 16], F32)
    B_tok = A.alloc("B_tok", [128, 4, 128], BF16)
    xdt_tok = A.alloc("xdt_tok", [128, 16, 64], BF16)
    xdtd = A.alloc("xdtd", [128, 16, 64], BF16)
    cbm = A.alloc("cbm", [128, 4, 128], BF16)
    RC_OFF = (A.off + 63) // 64 * 64
    rhs_cum = [None, None]
    rhs2 = [None, None]
    rhs_cum[0] = A.alloc("rhs_cum0", [128, 4, 128], F32)
    rhs2[0] = A.alloc("rhs2_0", [128, 4, 128], F32)
    rhs_cum[1] = A.alloc("rhs_cum1", [128, 4, 128], F32)
    rhs2[1] = A.alloc("rhs2_1", [128, 4, 128], F32)
    assert A.off == RC_OFF + 8192
    xbc_pre = [nc.alloc_sbuf_tensor_at(f"xbc_pre{i}", [128, 528], F32, offset=RC_OFF + 4096 * i) for i in range(2)]
    Eoff = [A.alloc(f"Eoff{i}", [128, 4, 128], F32) for i in range(2)]
    Cp_all = A.alloc("Cp_all", [128, 16, 128], BF16)
    E_t = [A.alloc(f"E_t{i}", [128, 4, 128], F32) for i in range(2)]
    MT = [A.alloc(f"MT{i}", [128, 4, 128], BF16) for i in range(2)]
    cd = A.alloc("cd", [128, 16], F32)
    YG_OFF = (A.off + 63) // 64 * 64
    yg = A.alloc("yg", [128, 8, 128], F32)
    assert A.off == YG_OFF + 4096
    dt_t1 = nc.alloc_sbuf_tensor_at("dt_t1", [16, 512], F32, offset=YG_OFF)
    dt_t2 = nc.alloc_sbuf_tensor_at("dt_t2", [16, 512], F32, offset=YG_OFF + 2048)
    y_sq = A.alloc("y_sq", [128, 8, 128], BF16)
    diagD = A.alloc("diagD", [128, 8, 128], BF16)
    g_rstd = A.alloc("g_rstd", [128, 4, 128], F32)
    SSDP0 = (A.off + 63) // 64 * 64
    tmpS = A.alloc("tmpS", [128, 4, 64], F32)
    SET1 = dict(dt_tok=A.alloc("dt_tok1", [128, 16], F32), da_tok=A.alloc("da_tok1", [128, 16], F32), nacum=A.alloc("nacum1", [128, 16], F32),
                B_tok=A.alloc("B_tok1", [128, 4, 128], BF16), xdt_tok=A.alloc("xdt_tok1", [128, 16, 64], BF16), cbm=A.alloc("cbm1", [128, 4, 128], BF16))
    SSDP_END = A.off
    A.off = SSDP0
    s0A = [yg, A.alloc("s0A1", [128, 8, 128], F32)]
    s0T_bf = [A.alloc(f"s0T_bf{i}", [128, 16, 64], BF16) for i in range(2)]
    Bm = [A.alloc(f"Bm{i}", [128, 4, 128], BF16) for i in range(2)]
    da_exp = nc.alloc_sbuf_tensor_at("da_exp_alias", [128, 16, 64], F32, offset=RC_OFF)
    cdA = A.alloc("cdA", [128, 8, 16], F32)
    csx = A.alloc("csx", [128, 16, NS, 3], F32)
    css = A.alloc("css", [128, 8, NS, 2], F32)
    SSD_END = max(A.off, SSDP_END)
    R_dtok = Res("dt_tok"); R_datok = Res("da_tok"); R_nacum = Res("nacum"); R_Btok = Res("B_tok")
    R_xdt = Res("xdt_tok"); R_xdtd = Res("xdtd"); R_cbm = Res("cbm")
    R_rc = [Res("rhs_cum0"), Res("rhs_cum1")]; R_r2 = [Res("rhs2_0"), Res("rhs2_1")]
    R_pre = [Res("pre0"), Res("pre1")]
    R_Eoff = [Res("Eoff0"), Res("Eoff1")]; R_Cp = [Res(f"Cp{g}") for g in range(4)]; R_E = [Res("E0"), Res("E1")]
    R_MT = [Res("MT0"), Res("MT1")]; R_cd = Res("cd")
    R_yg = Res("yg"); R_dt1 = R_yg; R_dt2 = R_yg; R_acc = [Res("acc0"), Res("acc1")]; R_ysq = Res("y_sq"); R_gv = Res("g_v"); R_diagD = Res("diagD"); R_grstd = Res("g_rstd")
    R_sT = [Res(f"sT{g}") for g in range(4)]; R_sTbf = [Res(f"sTbf{g}") for g in range(4)]; R_tmpS = Res("tmpS")
    R_s0A = [R_yg, Res("s0A1")]; R_s0T = [Res("s0T0"), Res("s0T1")]; R_Bm = [Res("Bm0"), Res("Bm1")]; R_daexp = Res("da_exp"); R_cdA = Res("cdA")
    R_csx = Res("csx"); R_css = Res("css")
    s0A_sem = [P.new_sem("s0A_sem0"), P.new_sem("s0A_sem1")]; s0T_sem = [P.new_sem("s0T_sem0"), P.new_sem("s0T_sem1")]; cs_sem = P.new_sem("cs_sem"); cs_sem2 = P.new_sem("cs_sem2")
    SET0 = dict(dt_tok=dt_tok, da_tok=da_tok, nacum=nacum, B_tok=B_tok, xdt_tok=xdt_tok, cbm=cbm)
    RSET = [dict(dt_tok=R_dtok, da_tok=R_datok, nacum=R_nacum, B_tok=R_Btok, xdt_tok=R_xdt, cbm=R_cbm),
            dict(dt_tok=Res("dt_tok1"), da_tok=Res("da_tok1"), nacum=Res("nacum1"), B_tok=Res("B_tok1"), xdt_tok=Res("xdt_tok1"), cbm=Res("cbm1"))]
    BSET = [SET0, SET1]
    SSD_RES = list(RSET[1].values()) + [R_dtok, R_datok, R_nacum, R_Btok, R_xdt, R_xdtd, R_cbm] + R_rc + R_r2 + R_Eoff + R_Cp + R_E + R_MT + [R_cd, R_diagD,
               R_yg, R_ysq, R_gv, R_grstd] + [R_tmpS, R_s0A[1]] + R_s0T + R_Bm + [R_daexp, R_cdA, R_csx, R_css]
    A.off = SSD0
    m1 = A.alloc("m1", [128, 8, 512], BF16)
    vT = A.alloc("vT", [128, 8, 512], BF16)
    ch_pre = [A.alloc(f"ch_pre{i}", [128, 520], F32) for i in range(2)]
    p_sg = [A.alloc(f"p_sg{i}", [128, 512], F32) for i in range(2)]
    p_c = [A.alloc(f"p_c{i}", [128, 512], F32) for i in range(2)]
    p_t = A.alloc("p_t", [128, 512], F32)
    p_u = A.alloc("p_u", [128, 512], F32)
    csc_out = A.alloc("csc_out", [128, 1024], F32)
    POST_END = A.off
    assert POST_END <= max(SSD_END, POST_END)
    R_m1 = [Res(f"m1_{m}") for m in range(8)]
    R_vT = Res("vT")
    R_chp = [Res("chp0"), Res("chp1")]
    R_psg = [Res("psg0"), Res("psg1")]
    R_pc = [Res("pc0"), Res("pc1")]
    R_pt = Res("p_t"); R_pu = Res("p_u"); R_csc = Res("csc_out")
    POST_RES = R_m1 + [R_vT] + R_chp + R_psg + R_pc + [R_pt, R_pu, R_csc]
    MIX_RES = [R_hT] + R_pre + R_xc + [R_sz, R_yn, R_dt] + R_sT + R_sTbf + SSD_RES + POST_RES
    MIX_END = max(SSD_END, POST_END)
    assert max(MIX_END, FFN_END) <= 229376, (MIX_END, FFN_END)
    print("SBUF: overlay base", OV, "FFN_END", FFN_END, "SSD0", SSD0, "SSD_END", SSD_END, "POST_END", POST_END)

    osem = {n: P.new_sem("o_" + n) for n in ("cs", "csc", "yg", "s0A0", "s0A1", "x")}
    setup_sem = P.new_sem("setup_sem")
    R_out = Res("out_dummy")

    def wload(src_ap, view):
        i = wctr[0] % NSLOT
        wctr[0] += 1
        dst = view(wslot[i])
        P.op("pool", lambda e, d=dst, s=src_ap: e.dma_start(out=d, in_=s), writes=[WR[i]], dma_sem=wsem[i])
        return wslot[i], WR[i]

    def wblk(wap, l, c0, ncols):
        assert ncols == WB and c0 % WB == 0
        src = wap[l, c0 // WB]
        t, r = wload(src, lambda s: s[:, 0:8 * ncols])
        return t[:, 0:8 * ncols].rearrange("p (kc n) -> p kc n", kc=8), r

    def mm_group(out_ap, pairs, out_res, in_res, last_inc=True, start=True, stop=True):
        n = len(pairs)
        for i, (l_, r_) in enumerate(pairs):
            P.op("pe", lambda e, o=out_ap, a=l_, b=r_, st=(start and i == 0), sp=(stop and i == n - 1):
                 e.matmul(o, lhsT=a, rhs=b, start=st, stop=sp),
                 reads=in_res, writes=[out_res], inc=(last_inc and i == n - 1))

    pbctr = {"a": 0, "b": 0, "c": 0}

    def rot(key, banks):
        b = banks[pbctr[key] % len(banks)]
        pbctr[key] += 1
        return b

    setup_res = []

    def sload(dst, src, res):
        setup_sem.val += 16
        P.q["sp"].append(("o", lambda e, d=dst, s=src: e.dma_start(out=d, in_=s), setup_sem, 16))
        setup_res.append(res)

    sload(consts[:], consts_in, R_const)
    sload(vecsT[:], vecsT_in, R_tab)
    sload(xbcv[:], xbcv_in, R_tab)
    sload(dtb[:], dtb_in, R_tab)
    sload(a_bc[:], alog_in, R_abc)
    sload(cT[:], cT_in, R_tab)
    sload(badaT[:], badaT_in, R_tab)
    xsrc = xT_in.rearrange("(kc p) t -> p kc t", p=128)
    for k, (t0, nt, _) in enumerate(TILES):
        sload(xT[:, :, t0:t0 + nt], xsrc[:, :, t0:t0 + nt], XR[k])
    for r in setup_res:
        r.w = (setup_sem, setup_sem.val)

    P.op("dve", lambda e: e.tensor_copy(out=ident_b[:], in_=ident_f), reads=[R_const], writes=[R_cb])
    P.op("dve", lambda e: e.tensor_copy(out=ones_b[:], in_=ones_f), reads=[R_const], writes=[R_cb])
    P.op("dve", lambda e: e.memset(mhalf[:], -0.5), writes=[R_cb])
    P.op("act", lambda e: e.activation(out=a_bc[:], in_=a_bc[:], func=AF.Exp), reads=[R_abc], writes=[R_abc])
    P.op("dve", lambda e: e.tensor_scalar(out=a_bc[:], in0=a_bc[:], scalar1=-1.0, scalar2=None, op0=ALU.mult),
         reads=[R_abc], writes=[R_abc])
    P.op("act", lambda e: e.activation(out=scT[:], in_=cT[:], func=AF.Silu), reads=[R_tab], writes=[R_scT])

    nbt = D // WB
    gains = {1: (lambda l: V_NF1(l)), 4: (lambda l: V_NMIX(l)), 7: (lambda l: V_NF2(l))}

    def mod_block(l, cb_):
        t = cb_ // nbt
        wt, wr = wblk(w_ada, l, cb_ * WB, WB)
        for cc in range(WB // 128):
            q = cb_ * (WB // 128) + cc
            b = 7
            mm_group(pb(b, 17), [(wt[:, kc, cc * 128:(cc + 1) * 128], scT[:, kc, :]) for kc in range(8)],
                     BANK[b], [wr, R_scT])
            P.op("dve", lambda e, q=q: e.tensor_scalar(out=modT[l][:, q, :], in0=pb(b, 17),
                 scalar1=badaT[:, l, q:q + 1], scalar2=None, op0=ALU.add),
                 reads=[BANK[b], R_tab], writes=[R_modt[l][t]])
        if cb_ % nbt == nbt - 1:
            v = modT[l][:, t * 8:(t + 1) * 8, :]
            if t in gains:
                vi = gains[t](l)
                P.op("dve", lambda e: e.tensor_scalar(out=v, in0=v, scalar1=1.0, scalar2=None, op0=ALU.add),
                     reads=[R_modt[l][t]], writes=[R_modt[l][t]])
                P.op("dve", lambda e: e.tensor_tensor(out=v, in0=v, in1=vecsT[:, :, vi:vi + 1].to_broadcast([128, 8, 17]), op=ALU.mult),
                     reads=[R_modt[l][t], R_tab], writes=[R_modt[l][t]])
            elif t in (2, 8):
                P.op("dve", lambda e: e.tensor_scalar(out=v, in0=v, scalar1=0.5, scalar2=None, op0=ALU.mult),
                     reads=[R_modt[l][t]], writes=[R_modt[l][t]])

    mod_q = [(l, cb_) for l in range(2) for cb_ in range(9 * nbt)]
    mod_done = [0]

    def sprinkle(n=1):
        for _ in range(n):
            if mod_done[0] < len(mod_q):
                mod_block(*mod_q[mod_done[0]])
                mod_done[0] += 1

    def mod_ensure(l, t):
        need = (l * 9 + t + 1) * nbt
        while mod_done[0] < need:
            sprinkle()

    mod_ensure(0, 2)

    def modA(l, i):
        return modT[l][:, (3 * i + 1) * 8:(3 * i + 2) * 8, :]

    def modB(l, i):
        return modT[l][:, (3 * i) * 8:(3 * i + 1) * 8, :]

    def modG(l, i):
        return modT[l][:, (3 * i + 2) * 8:(3 * i + 3) * 8, :]

    def rstd_from_psum(b, nt, scale, v_ap, rstd_ap, r_v, r_rstd):
        P.op("act", lambda e: e.activation(out=v_ap, in_=pb(b, nt), func=AF.Ln, bias=EPS, scale=scale),
             reads=[BANK[b]], writes=[r_v])
        P.op("act", lambda e: e.activation(out=rstd_ap, in_=v_ap, func=AF.Exp, scale=-0.5),
             reads=[r_v], writes=[r_rstd])

    def norm_mod(l, i, k, hdst, r_h):
        t0, nt, samp = TILES[k]
        b = 7
        for kc in range(8):
            s = kc % 2
            P.op("act", lambda e, kc=kc, s=s: e.activation(out=n_sq[s][:, 0:nt], in_=xT[:, kc, t0:t0 + nt], func=AF.Square),
                 reads=[XR[k]], writes=[R_nsq[s]])
            P.op("pe", lambda e, kc=kc, s=s: e.matmul(pb(b, nt), lhsT=ones_b[:], rhs=n_sq[s][:, 0:nt], start=(kc == 0), stop=(kc == 7)),
                 reads=[R_nsq[s], R_cb], writes=[BANK[b]], inc=True)
        rstd_from_psum(b, nt, 1.0 / D, n_rstd[:, 0:nt], n_rstd[:, 0:nt], R_nrstd, R_nrstd)
        Am, Bm_ = modA(l, i), modB(l, i)
        for kc in range(8):
            s = kc % 2
            P.op("dve", lambda e, kc=kc, s=s: e.tensor_tensor(out=n_tmp[s][:, 0:nt], in0=xT[:, kc, t0:t0 + nt], in1=n_rstd[:, 0:nt], op=ALU.mult),
                 reads=[XR[k], R_nrstd], writes=[R_ntmp[s]])
            if not samp:
                P.op("act", lambda e, kc=kc, s=s: e.activation(out=hdst[:, kc, :], in_=n_tmp[s][:, 0:nt], func=AF.Identity,
                     bias=Bm_[:, kc, 0:1], scale=Am[:, kc, 0:1]), reads=[R_ntmp[s], R_modt[l][3 * i], R_modt[l][3 * i + 1]], writes=[r_h])
            else:
                tv = n_tmp[s][:, 0:nt].rearrange("p (b t) -> p b t", t=LS)
                P.op("dve", lambda e, kc=kc, tv=tv: e.tensor_tensor(out=tv, in0=tv,
                     in1=Am[:, kc, 1:17].unsqueeze(2).to_broadcast([128, NS, LS]), op=ALU.mult),
                     reads=[R_ntmp[s], R_modt[l][3 * i + 1]], writes=[R_ntmp[s]])
                P.op("dve", lambda e, kc=kc, tv=tv: e.tensor_tensor(out=hdst[:, kc, :].rearrange("p (b t) -> p b t", t=LS), in0=tv,
                     in1=Bm_[:, kc, 1:17].unsqueeze(2).to_broadcast([128, NS, LS]), op=ALU.add),
                     reads=[R_ntmp[s], R_modt[l][3 * i]], writes=[r_h])

    def resid_update(l, i, k, m, b, nt, t0, samp, tmp_ap, r_tmp):
        G = modG(l, i)
        if not samp:
            P.op("dve", lambda e: e.scalar_tensor_tensor(out=xT[:, m, t0:t0 + nt], in0=pb(b, nt), scalar=G[:, m, 0:1],
                 in1=xT[:, m, t0:t0 + nt], op0=ALU.mult, op1=ALU.add), reads=[BANK[b], R_modt[l][3 * i + 2], XR[k]], writes=[XR[k]])
        else:
            tv = tmp_ap[:, 0:nt].rearrange("p (b t) -> p b t", t=LS)
            P.op("dve", lambda e: e.tensor_tensor(out=tv, in0=pb(b, nt).rearrange("p (b t) -> p b t", t=LS),
                 in1=G[:, m, 1:17].unsqueeze(2).to_broadcast([128, NS, LS]), op=ALU.mult),
                 reads=[BANK[b], R_modt[l][3 * i + 2]], writes=[r_tmp])
            P.op("dve", lambda e: e.tensor_tensor(out=xT[:, m, t0:t0 + nt], in0=xT[:, m, t0:t0 + nt], in1=tmp_ap[:, 0:nt], op=ALU.add),
                 reads=[r_tmp, XR[k]], writes=[XR[k]])

    def ffn_stage(l, which, tiles_):
        i = 0 if which == 0 else 2
        offs = []
        o = 0
        for k in tiles_:
            offs.append(o)
            o += TILES[k][1]
        for si, k in enumerate(tiles_):
            nt = TILES[k][1]
            norm_mod(l, i, k, hT_st[:, :, offs[si]:offs[si] + nt], R_hTst[si])
        wgu = w_gu[which]
        nblk = DFF // WB
        for jb in range(nblk):
            wg, rg = wblk(wgu, l, jb * WB, WB)
            wu, ru = wblk(wgu, l, DFF + jb * WB, WB)
            for jj in range(WB // 128):
                j = jb * (WB // 128) + jj
                for si, k in enumerate(tiles_):
                    nt = TILES[k][1]
                    o0 = offs[si]
                    bg = rot("a", [0, 1])
                    bu = rot("b", [2, 3])
                    mm_group(pb(bg, nt), [(wg[:, kc, jj * 128:(jj + 1) * 128], hT_st[:, kc, o0:o0 + nt]) for kc in range(8)],
                             BANK[bg], [rg, R_hTst[si]])
                    mm_group(pb(bu, nt), [(wu[:, kc, jj * 128:(jj + 1) * 128], hT_st[:, kc, o0:o0 + nt]) for kc in range(8)],
                             BANK[bu], [ru, R_hTst[si]])
                    s = pbctr["c"] % 2
                    pbctr["c"] += 1
                    P.op("act", lambda e, s=s, bg=bg, nt=nt: e.activation(out=f_sg[s][:, 0:nt], in_=pb(bg, nt), func=AF.Silu),
                         reads=[BANK[bg]], writes=[R_fsg[s]])
                    P.op("dve", lambda e, s=s, bu=bu, nt=nt, j=j, o0=o0: e.tensor_tensor(out=hid[:, j, o0:o0 + nt], in0=f_sg[s][:, 0:nt],
                         in1=pb(bu, nt), op=ALU.mult), reads=[R_fsg[s], BANK[bu]], writes=[R_hid[si]])
            sprinkle(1)
        wdn = w_dn[which]
        for m in range(8):
            s = m % 3
            src = wdn[l, m]
            P.op("pool", lambda e, s=s, src=src: e.dma_start(out=wdq[s][:].rearrange("p j n -> p (j n)"), in_=src), writes=[R_wdq[s]], dma_sem=wdsem[s])
            for si, k in enumerate(tiles_):
                t0, nt, samp = TILES[k]
                o0 = offs[si]
                b = rot("c", [4, 5, 6])
                mm_group(pb(b, nt), [(wdq[s][:, j, :], hid[:, j, o0:o0 + nt]) for j in range(NJ)], BANK[b], [R_wdq[s], R_hid[si]])
                resid_update(l, i, k, m, b, nt, t0, samp, f_sg[0], R_fsg[0])
            sprinkle(1)

    def proj_chunks(wap, l, c0, nchunks, hsrc, r_hsrc, nt, consume):
        nb = (nchunks * 128 + WB - 1) // WB
        for bi in range(nb):
            wt, wr = wblk(wap, l, c0 + bi * WB, WB)
            for cc in range(WB // 128):
                ci = bi * (WB // 128) + cc
                b = rot("a", [0, 1, 2, 3, 4, 5, 6, 7])
                mm_group(pb(b, nt), [(wt[:, kc, cc * 128:(cc + 1) * 128], hsrc[:, kc, 0:nt]) for kc in range(8)],
                         BANK[b], [wr, r_hsrc])
                consume(ci, b, wt, wr, cc)

    def mix_stage(l, k):
        t0, nt, samp = TILES[k]
        first = (k == 0)
        v = 1 if samp else 0
        nchunk = nt // 128
        norm_mod(l, 1, k, hT[:, :, 0:nt], R_hT)
        if samp:
            P.op("sp", lambda e: e.dma_start(out=csx[:], in_=csx_in[l]), writes=[R_csx], dma_sem=cs_sem)
            P.op("sp", lambda e: e.dma_start(out=css[:], in_=css_in[l]), writes=[R_css], dma_sem=cs_sem2)
        if first:
            P.op("dve", lambda e: e.memset(hist_x[:], 0.0), writes=[R_hx])
            P.op("dve", lambda e: e.memset(hist_s[:], 0.0), writes=[R_hs])
        want_cs = (k == 3) or samp
        csv = csout[:].rearrange("p a b -> p (a b)").bitcast(F32)

        handoff(R_rc + R_r2, R_pre)
        handoff([R_yg], R_acc)
        pend_silu = []

        def consume_xbc(m, b, wt, wr, cc):
            s = m % 2
            pre = xbc_pre[s]
            if not samp:
                P.op("act", lambda e: e.copy(out=pre[:, 0:3], in_=hist_x[:, m, :]), reads=[R_hx], writes=[R_pre[s]])
                P.op("act", lambda e: e.copy(out=pre[:, 3:3 + nt], in_=pb(b, nt)), reads=[BANK[b]], writes=[R_pre[s]])
                P.op("act", lambda e: e.copy(out=hist_x[:, m, :], in_=pre[:, nt:nt + 3]), reads=[R_pre[s]], writes=[R_hx])
                ext = lambda kk: pre[:, kk:kk + nt]
                acc = pre[:, 0:nt]
                accv = acc
            else:
                pv = pre[:, 0:NS * 11].rearrange("p (b t) -> p b t", t=11)
                P.op("act", lambda e: e.copy(out=pv[:, :, 0:3], in_=csx[:, m, :, :]), reads=[R_csx], writes=[R_pre[s]])
                P.op("act", lambda e: e.copy(out=pv[:, :, 3:11], in_=pb(b, nt).rearrange("p (b t) -> p b t", t=LS)),
                     reads=[BANK[b]], writes=[R_pre[s]])
                ext = lambda kk: pv[:, :, kk:kk + LS]
            cacc = yg[:].rearrange("p a b -> p (a b)")[:, 512 * s:512 * s + nt]
            caccv = cacc if not samp else cacc.rearrange("p (b t) -> p b t", t=LS)
            P.op("act", lambda e: e.activation(out=cacc, in_=pb(b, nt), func=AF.Identity, scale=xbcv[:, m, X_CW(l, 3):X_CW(l, 3) + 1]),
                 reads=[BANK[b], R_tab], writes=[R_acc[s]])
            for kk in range(3):
                P.op("dve", lambda e, kk=kk: e.scalar_tensor_tensor(out=caccv, in0=ext(kk), scalar=xbcv[:, m, X_CW(l, kk):X_CW(l, kk) + 1],
                     in1=caccv, op0=ALU.mult, op1=ALU.add), reads=[R_pre[s], R_tab, R_acc[s]], writes=[R_acc[s]])
            def silu_m():
                P.op("act", lambda e: e.activation(out=xbc_c[:, m, 0:nt], in_=cacc, func=AF.Silu, bias=xbcv[:, m, X_CB(l):X_CB(l) + 1]),
                     reads=[R_acc[s], R_tab], writes=[R_xc[m]])
            if pend_silu:
                pend_silu.pop()()
            pend_silu.append(silu_m)
            if want_cs and cc == WB // 128 - 1:
                c0 = (m // (WB // 128)) * WB
                bb = rot("a", [0, 1, 2, 3, 4, 5, 6, 7])
                if samp:
                    mtok, tsl = 128, slice(0, 128)
                else:
                    mtok, tsl = 3, slice(nt - 3, nt)
                mm_group(psum[0:mtok, bb * 512:bb * 512 + WB], [(hT[:, kc, tsl], wt[:, kc, :]) for kc in range(8)], BANK[bb], [wr, R_hT])
                P.op("act", lambda e: e.copy(out=csv[0:mtok, c0:c0 + WB], in_=psum[0:mtok, bb * 512:bb * 512 + WB]),
                     reads=[BANK[bb]], writes=[R_yn])

        proj_chunks(w_in, l, C_XBC, 16, hT, R_hT, nt, consume_xbc)
        pend_silu.pop()()
        handoff(R_acc, [R_yg])
        if want_cs:
            if not samp:
                P.op("sp", lambda e: e.dma_start(out=csx_p_out[l], in_=csv[0:3, :]), reads=[R_yn], writes=[R_out], dma_sem=osem["cs"])
            else:
                for r in range(3):
                    P.op("sp", lambda e, r=r: e.dma_start(out=csx_s_out[l, :, r, :], in_=csv[5 + r:128:8, :]),
                         reads=[R_yn], writes=[R_out], dma_sem=osem["cs"])

        wt_, wr_ = wload(w_dt[l], lambda s_: s_[:, 0:128])
        wdt = wt_[:, 0:128].rearrange("p (kc n) -> p kc n", kc=8)
        b = rot("a", [0, 1, 2, 3, 4, 5, 6, 7])
        mm_group(psum[0:16, b * 512:b * 512 + nt], [(wdt[:, kc, :], hT[:, kc, 0:nt]) for kc in range(8)], BANK[b], [wr_, R_hT])
        P.op("act", lambda e: e.activation(out=dt_t1[:, 0:nt], in_=psum[0:16, b * 512:b * 512 + nt], func=AF.Identity, bias=dtb[:, l:l + 1]),
             reads=[BANK[b], R_tab], writes=[R_dt1])
        P.op("act", lambda e: e.activation(out=dt_t2[:, 0:nt], in_=dt_t1[:, 0:nt], func=AF.Abs), reads=[R_dt1], writes=[R_dt2])
        P.op("act", lambda e: e.activation(out=dt_t2[:, 0:nt], in_=dt_t2[:, 0:nt], func=AF.Exp, scale=-1.0), reads=[R_dt2], writes=[R_dt2])
        P.op("act", lambda e: e.activation(out=dt_t2[:, 0:nt], in_=dt_t2[:, 0:nt], func=AF.Ln, bias=1.0), reads=[R_dt2], writes=[R_dt2])
        P.op("dve", lambda e: e.tensor_scalar(out=dt_t1[:, 0:nt], in0=dt_t1[:, 0:nt], scalar1=0.0, scalar2=None, op0=ALU.max),
             reads=[R_dt1], writes=[R_dt1])
        P.op("dve", lambda e: e.tensor_tensor(out=dtT[:, 0:nt], in0=dt_t1[:, 0:nt], in1=dt_t2[:, 0:nt], op=ALU.add),
             reads=[R_dt1, R_dt2], writes=[R_dt])

        def consume_z(m, b, wt, wr, cc):
            P.op("act", lambda e: e.activation(out=sz[:, m, 0:nt], in_=pb(b, nt), func=AF.Silu), reads=[BANK[b]], writes=[R_sz])
        proj_chunks(w_in, l, C_Z, 8, hT, R_hT, nt, consume_z)

        handoff(R_pre, R_rc + R_r2)
        for j in range(8):
            P.op("dve", lambda e, j=j: e.tensor_scalar(out=diagD[:, j, :], in0=ident_f, scalar1=vecsT[:, j, V_DP(l):V_DP(l) + 1], scalar2=None, op0=ALU.mult),
                 reads=[R_const, R_tab], writes=[R_diagD])
        ssd_prologue(l, k, 0, samp, v)
        for ci in range(nchunk):
            nxt = (lambda c=ci + 1: ssd_prologue(l, k, c, samp, v)) if ci + 1 < nchunk else None
            ssd_chunk(l, k, ci, samp, v, nxt)
        if pend_evac:
            pend_evac.pop()()

        handoff(SSD_RES, POST_RES)

        gate_banks = {}

        def consume_gate(m, b, wt, wr, cc):
            s = m % 2
            P.op("act", lambda e: e.activation(out=p_sg[s][:, 0:nt], in_=pb(b, nt), func=AF.Sigmoid), reads=[BANK[b]], writes=[R_psg[s]])

        def consume_yssd(m, b, wt, wr, cc):
            s = m % 2
            P.op("dve", lambda e: e.tensor_tensor(out=m1[:, m, 0:nt], in0=p_sg[s][:, 0:nt], in1=pb(b, nt), op=ALU.mult),
                 reads=[R_psg[s], BANK[b]], writes=[R_m1[m]])

        for half in range(8 // (WB // 128)):
            nn = WB // 128
            proj_sub(w_in, l, C_GSSD + half * WB, hT, R_hT, nt, lambda ci, b, wt, wr, cc, h_=half: consume_gate(h_ * nn + ci, b, wt, wr, cc))
            proj_sub(w_out_ssd, l, half * WB, ynT, R_yn, nt, lambda ci, b, wt, wr, cc, h_=half: consume_yssd(h_ * nn + ci, b, wt, wr, cc))

        want_cs2 = want_cs
        for half in range(8 // (WB // 128)):
            nn = WB // 128

            def consume_c(ci, b, wt, wr, cc, h_=half):
                s = (h_ * nn + ci) % 2
                P.op("act", lambda e: e.copy(out=p_c[s][:, 0:nt], in_=pb(b, nt)), reads=[BANK[b]], writes=[R_pc[s]])
                if want_cs2 and cc == nn - 1:
                    tok_mm(wt, wr, "c", h_)

            def consume_h(ci, b, wt, wr, cc, h_=half):
                m = h_ * nn + ci
                s = m % 2
                pre = ch_pre[s]
                if not samp:
                    P.op("act", lambda e: e.copy(out=pre[:, 0:2], in_=hist_s[:, m, :]), reads=[R_hs], writes=[R_chp[s]])
                    P.op("dve", lambda e: e.tensor_tensor(out=pre[:, 2:2 + nt], in0=p_c[s][:, 0:nt], in1=pb(b, nt), op=ALU.mult),
                         reads=[R_pc[s], BANK[b]], writes=[R_chp[s]])
                    P.op("act", lambda e: e.copy(out=hist_s[:, m, :], in_=pre[:, nt:nt + 2]), reads=[R_chp[s]], writes=[R_hs])
                    ext = lambda kk: pre[:, kk:kk + nt]
                    uv = p_u[:, 0:nt]
                else:
                    pv = pre[:, 0:NS * 10].rearrange("p (b t) -> p b t", t=10)
                    P.op("act", lambda e: e.copy(out=pv[:, :, 0:2], in_=css[:, m, :, :]), reads=[R_css], writes=[R_chp[s]])
                    P.op("dve", lambda e: e.tensor_tensor(out=pv[:, :, 2:10], in0=p_c[s][:, 0:nt].rearrange("p (b t) -> p b t", t=LS),
                         in1=pb(b, nt).rearrange("p (b t) -> p b t", t=LS), op=ALU.mult), reads=[R_pc[s], BANK[b]], writes=[R_chp[s]])
                    ext = lambda kk: pv[:, :, kk:kk + LS]
                    uv = p_u[:, 0:nt].rearrange("p (b t) -> p b t", t=LS)
                P.op("dve", lambda e: e.tensor_scalar(out=uv, in0=ext(0), scalar1=vecsT[:, m, V_SCW(l, 0):V_SCW(l, 0) + 1], scalar2=None, op0=ALU.mult),
                     reads=[R_chp[s], R_tab], writes=[R_pu])
                for kk in range(1, 3):
                    P.op("dve", lambda e, kk=kk: e.scalar_tensor_tensor(out=uv, in0=ext(kk), scalar=vecsT[:, m, V_SCW(l, kk):V_SCW(l, kk) + 1],
                         in1=uv, op0=ALU.mult, op1=ALU.add), reads=[R_chp[s], R_tab, R_pu], writes=[R_pu])
                if want_cs2 and cc == nn - 1:
                    tok_mm(wt, wr, "h", h_)

            def consume_b(ci, b, wt, wr, cc, h_=half):
                pass

            def tok_mm(wt, wr, which_, h_):
                bb = rot("a", [0, 1, 2, 3, 4, 5, 6, 7])
                if samp:
                    mtok, tsl = 128, slice(0, 128)
                else:
                    mtok, tsl = 2, slice(nt - 2, nt)
                mm_group(psum[0:mtok, bb * 512:bb * 512 + WB], [(hT[:, kc, tsl], wt[:, kc, :]) for kc in range(8)], BANK[bb], [wr, R_hT])
                dst = csc_out[0:mtok, h_ * WB:(h_ + 1) * WB]
                if which_ == "c":
                    P.op("act", lambda e: e.copy(out=dst, in_=psum[0:mtok, bb * 512:bb * 512 + WB]), reads=[BANK[bb]], writes=[R_csc])
                else:
                    P.op("dve", lambda e: e.tensor_tensor(out=dst, in0=dst, in1=psum[0:mtok, bb * 512:bb * 512 + WB], op=ALU.mult),
                         reads=[BANK[bb], R_csc], writes=[R_csc])

            wc, rc_ = wblk(w_in, l, C_SCC + half * WB, WB)
            wh, rh_ = wblk(w_in, l, C_SCH + half * WB, WB)
            wb_, rb_ = wblk(w_in, l, C_SCB + half * WB, WB)
            for cc in range(nn):
                m = half * nn + cc
                b1 = rot("a", [0, 1, 2, 3, 4, 5, 6, 7])
                mm_group(pb(b1, nt), [(wc[:, kc, cc * 128:(cc + 1) * 128], hT[:, kc, 0:nt]) for kc in range(8)], BANK[b1], [rc_, R_hT])
                consume_c(cc, b1, wc, rc_, cc)
                b2 = rot("a", [0, 1, 2, 3, 4, 5, 6, 7])
                mm_group(pb(b2, nt), [(wh[:, kc, cc * 128:(cc + 1) * 128], hT[:, kc, 0:nt]) for kc in range(8)], BANK[b2], [rh_, R_hT])
                consume_h(cc, b2, wh, rh_, cc)
                b3 = rot("a", [0, 1, 2, 3, 4, 5, 6, 7])
                mm_group(pb(b3, nt), [(wb_[:, kc, cc * 128:(cc + 1) * 128], hT[:, kc, 0:nt]) for kc in range(8)], BANK[b3], [rb_, R_hT])
                P.op("dve", lambda e, m=m, b3=b3: e.tensor_tensor(out=vT[:, m, 0:nt], in0=p_u[:, 0:nt], in1=pb(b3, nt), op=ALU.mult),
                     reads=[R_pu, BANK[b3]], writes=[R_vT])
        if want_cs2:
            if not samp:
                P.op("sp", lambda e: e.dma_start(out=css_p_out[l], in_=csc_out[0:2, :]), reads=[R_csc], writes=[R_out], dma_sem=osem["csc"])
            else:
                for r in range(2):
                    P.op("sp", lambda e, r=r: e.dma_start(out=css_s_out[l, :, r, :], in_=csc_out[6 + r:128:8, :]),
                         reads=[R_csc], writes=[R_out], dma_sem=osem["csc"])

        def consume_gate2(m, b, wt, wr, cc):
            s = m % 2
            P.op("act", lambda e: e.activation(out=p_sg[s][:, 0:nt], in_=pb(b, nt), func=AF.Sigmoid), reads=[BANK[b]], writes=[R_psg[s]])

        def consume_ysc(m, b, wt, wr, cc):
            s = m % 2
            P.op("dve", lambda e: e.tensor_tensor(out=p_t[:, 0:nt], in0=p_sg[s][:, 0:nt], in1=pb(b, nt), op=ALU.mult),
                 reads=[R_psg[s], BANK[b]], writes=[R_pt])
            P.op("dve", lambda e: e.tensor_tensor(out=m1[:, m, 0:nt], in0=m1[:, m, 0:nt], in1=p_t[:, 0:nt], op=ALU.add),
                 reads=[R_pt, R_m1[m]], writes=[R_m1[m]])

        for half in range(8 // (WB // 128)):
            nn = WB // 128
            proj_sub(w_in, l, C_GSC + half * WB, hT, R_hT, nt, lambda ci, b, wt, wr, cc, h_=half: consume_gate2(h_ * nn + ci, b, wt, wr, cc))
            proj_sub(w_out_sc, l, half * WB, vT, R_vT, nt, lambda ci, b, wt, wr, cc, h_=half: consume_ysc(h_ * nn + ci, b, wt, wr, cc))

        def consume_o(m, b, wt, wr, cc):
            resid_update(l, 1, k, m, b, nt, t0, samp, p_t, R_pt)
        proj_chunks_multi(w_o, l, 0, 8, m1, R_m1, nt, consume_o)

        handoff(POST_RES, SSD_RES)

    def proj_sub(wap, l, c0, hsrc, r_hsrc, nt, consume):
        proj_chunks(wap, l, c0, WB // 128, hsrc, r_hsrc, nt, consume)

    def proj_chunks_multi(wap, l, c0, nchunks, hsrc, r_list, nt, consume):
        nb = (nchunks * 128) // WB
        for bi in range(nb):
            wt, wr = wblk(wap, l, c0 + bi * WB, WB)
            for cc in range(WB // 128):
                ci = bi * (WB // 128) + cc
                b = rot("a", [0, 1, 2, 3, 4, 5, 6, 7])
                mm_group(pb(b, nt), [(wt[:, kc, cc * 128:(cc + 1) * 128], hsrc[:, kc, 0:nt]) for kc in range(8)],
                         BANK[b], [wr] + list(r_list))
                consume(ci, b, wt, wr, cc)

    pend_evac = []

    def ssd_prologue(l, k, ci, samp, v):
        t0, nt, _ = TILES[k]
        c0 = ci * 128
        cols = slice(c0, c0 + 128)
        a_l = a_bc[:, 16 * l:16 * l + 16]
        pq = ci % 2
        dt_tok, da_tok, nacum, B_tok, xdt_tok, cbm = (BSET[pq][n] for n in ("dt_tok", "da_tok", "nacum", "B_tok", "xdt_tok", "cbm"))
        R_dtok, R_datok, R_nacum, R_Btok, R_xdt, R_cbm = (RSET[pq][n] for n in ("dt_tok", "da_tok", "nacum", "B_tok", "xdt_tok", "cbm"))
        xtok_ps = pb16(4).rearrange("p (m f) -> p m f", m=8)
        for m in range(8):
            P.op("pe", lambda e, m=m: e.transpose(xtok_ps[:, m, :], xbc_c[:, m, cols], ident_b[:]),
                 reads=[R_xc[m], R_cb], writes=[BANK[4]], inc=(m == 7))
        btok_ps = pb16(5, 512).rearrange("p (g f) -> p g f", g=4)
        for g in range(4):
            P.op("pe", lambda e, g=g: e.transpose(btok_ps[:, g, :], xbc_c[:, 8 + g, cols], ident_b[:]),
                 reads=[R_xc[8 + g], R_cb], writes=[BANK[5]], inc=False)
        dttok_ps = pb(5, 16, 256)
        acum_ps = pb(5, 16, 288)
        P.op("pe", lambda e: e.matmul(dttok_ps, lhsT=dtT[:, cols], rhs=ident_f[0:16, 0:16], start=True, stop=True),
             reads=[R_dt, R_const], writes=[BANK[5]], inc=True)
        P.op("dve", lambda e: e.tensor_copy(out=dt_tok[:], in_=dttok_ps), reads=[BANK[5]], writes=[R_dtok])
        P.op("dve", lambda e: e.tensor_tensor(out=da_tok[:], in0=dt_tok[:], in1=a_l, op=ALU.mult), reads=[R_dtok, R_abc], writes=[R_datok])
        P.op("act", lambda e: e.copy(out=B_tok[:], in_=btok_ps), reads=[BANK[5]], writes=[R_Btok])
        P.op("dve", lambda e: e.tensor_tensor(out=xdt_tok[:], in0=pb16(4).rearrange("p (h f) -> p h f", h=16),
             in1=dt_tok[:].unsqueeze(2).to_broadcast([128, 16, 64]), op=ALU.mult), reads=[BANK[4], R_dtok], writes=[R_xdt])
        P.op("pe", lambda e: e.matmul(acum_ps, lhsT=U_f[v], rhs=da_tok[:], start=True, stop=True),
             reads=[R_datok, R_const], writes=[BANK[5]], inc=True)
        P.op("dve", lambda e: e.tensor_copy(out=nacum[:], in_=acum_ps), reads=[BANK[5]], writes=[R_nacum])
        cb_ps = pb(6).rearrange("p (g f) -> p g f", g=4)
        for g in range(4):
            P.op("pe", lambda e, g=g: e.matmul(cb_ps[:, g, :], lhsT=xbc_c[:, 8 + g, cols], rhs=xbc_c[:, 12 + g, cols], start=True, stop=True),
                 reads=[R_xc[8 + g], R_xc[12 + g]], writes=[BANK[6]], inc=(g == 3))
        P.op("dve", lambda e: e.tensor_tensor(out=cbm[:], in0=cb_ps, in1=M01_f[v].unsqueeze(1).to_broadcast([128, 4, 128]), op=ALU.mult),
             reads=[BANK[6], R_const], writes=[R_cbm])

    def ssd_chunk(l, k, ci, samp, v, next_prologue=None):
        t0, nt, _ = TILES[k]
        c0 = ci * 128
        cols = slice(c0, c0 + 128)
        gchunk = (t0 // 128 + ci) if not samp else 0
        has_state = (not samp) and gchunk > 0
        pq = ci % 2
        dt_tok, da_tok, nacum, B_tok, xdt_tok, cbm = (BSET[pq][n] for n in ("dt_tok", "da_tok", "nacum", "B_tok", "xdt_tok", "cbm"))
        R_dtok, R_datok, R_nacum, R_Btok, R_xdt, R_cbm = (RSET[pq][n] for n in ("dt_tok", "da_tok", "nacum", "B_tok", "xdt_tok", "cbm"))
        if pend_evac:
            pend_evac.pop()()
        y_ps = psum[:, 2 * 512:4 * 512].rearrange("p (j f) -> p j f", j=8)
        if samp:
            P.op("dve", lambda e: e.memset(psum[:, 2 * 512:4 * 512], 0.0), writes=[BANK[2], BANK[3]])
        def s1a(g):
            rb = g % 2
            hs = slice(4 * g, 4 * g + 4)
            P.op("pool", lambda e: e.tensor_tensor(out=rhs_cum[rb][:], in0=U_f[v].unsqueeze(1).to_broadcast([128, 4, 128]),
                 in1=da_tok[:, hs].unsqueeze(2).to_broadcast([128, 4, 128]), op=ALU.mult), reads=[R_datok, R_const], writes=[R_rc[rb]])
            P.op("pool", lambda e: e.tensor_tensor(out=rhs2[rb][:], in0=MB_f[v].unsqueeze(1).to_broadcast([128, 4, 128]),
                 in1=nacum[:, hs].unsqueeze(2).to_broadcast([128, 4, 128]), op=ALU.subtract), reads=[R_nacum, R_const], writes=[R_r2[rb]])
            P.op("pe", lambda e: e.matmul(pb(rb), lhsT=ones_f, rhs=rhs_cum[rb][:].rearrange("p a b -> p (a b)"), start=True, stop=True),
                 reads=[R_rc[rb], R_const], writes=[BANK[rb]], inc=True)

        def s1b_pair(ga, gb):
            def eoff(g):
                rb = g % 2
                R_ps = pb(rb).rearrange("p (h f) -> p h f", h=4)
                P.op("act", lambda e: e.activation(out=Eoff[rb][:], in_=R_ps, func=AF.Exp), reads=[BANK[rb]], writes=[R_Eoff[rb]])
                P.op("dve", lambda e: e.tensor_tensor(out=E_t[rb][:], in0=R_ps, in1=rhs2[rb][:], op=ALU.add),
                     reads=[BANK[rb], R_r2[rb]], writes=[R_E[rb]])

            def expe(g):
                rb = g % 2
                P.op("act", lambda e: e.activation(out=E_t[rb][:], in_=E_t[rb][:], func=AF.Exp), reads=[R_E[rb]], writes=[R_E[rb]])
            eoff(ga)
            eoff(gb)
            expe(ga)
            expe(gb)

        def stage2(g):
            rb = g % 2
            hs = slice(4 * g, 4 * g + 4)
            P.op("dve", lambda e: e.tensor_tensor(out=MT[rb][:], in0=E_t[rb][:], in1=cbm[:, g, :].unsqueeze(1).to_broadcast([128, 4, 128]), op=ALU.mult),
                 reads=[R_E[rb], R_cbm], writes=[R_MT[rb]])
            P.op("pool", lambda e: e.tensor_tensor(out=Cp_all[:, hs, :], in0=Eoff[rb][:],
                 in1=xbc_c[:, 12 + g, cols].unsqueeze(1).to_broadcast([128, 4, 128]), op=ALU.mult),
                 reads=[R_Eoff[rb], R_xc[12 + g]], writes=[R_Cp[g]])
            if not samp:
                P.op("dve", lambda e: e.tensor_copy(out=cd[:, hs], in_=Eoff[rb][:, :, 127]), reads=[R_Eoff[rb]], writes=[R_cd])
                P.op("dve", lambda e: e.tensor_tensor(out=xdtd[:, hs, :], in0=xdt_tok[:, hs, :],
                     in1=E_t[rb][:, :, 127].unsqueeze(2).to_broadcast([128, 4, 64]), op=ALU.mult), reads=[R_E[rb], R_xdt], writes=[R_xdtd])
            else:
                P.op("dve", lambda e: e.tensor_tensor(out=E_t[rb][:], in0=E_t[rb][:], in1=sel_f.unsqueeze(1).to_broadcast([128, 4, 128]), op=ALU.mult),
                     reads=[R_E[rb], R_const, R_MT[rb]], writes=[R_E[rb]])
                P.op("dve", lambda e: e.tensor_reduce(out=cd[:, hs], in_=E_t[rb][:], axis=mybir.AxisListType.X, op=ALU.add),
                     reads=[R_E[rb]], writes=[R_cd])
                P.op("dve", lambda e: e.tensor_tensor(out=xdtd[:, hs, :], in0=xdt_tok[:, hs, :],
                     in1=cd[:, hs].unsqueeze(2).to_broadcast([128, 4, 64]), op=ALU.mult), reads=[R_cd, R_xdt], writes=[R_xdtd])
            for hh in range(4):
                h = 4 * g + hh
                j, half = h // 2, h % 2
                yb = 2 + (j // 4)
                osl = y_ps[64 * half:64 * half + 64, j, :]
                last = (not has_state) and (not samp)
                if half == 0:
                    P.op("pe", lambda e, j=j: e.matmul(y_ps[:, j, :], lhsT=diagD[:, j, :], rhs=xbc_c[:, j, cols], start=(not samp), stop=False, skip_group_check=samp),
                         reads=[R_diagD, R_xc[j]], writes=[BANK[yb]], inc=False)
                P.op("pe", lambda e, h=h, hh=hh, osl=osl, last=last: e.matmul(osl, lhsT=xdt_tok[:, h, :], rhs=MT[rb][:, hh, :], start=False, stop=last, skip_group_check=samp),
                     reads=[R_xdt, R_MT[rb]], writes=[BANK[yb]], inc=(hh == 3))
                if has_state:
                    P.op("pe", lambda e, h=h, hh=hh, osl=osl, half=half: e.matmul(osl, lhsT=sT_bf[:, h, :], rhs=Cp_all[:, h, :], start=False, stop=True),
                         reads=[R_sTbf[g], R_Cp[g]], writes=[BANK[yb]], inc=(hh == 3))
            if not samp:
                S_ps = pb(7, 256, 256 * (g % 2))
                P.op("pe", lambda e, S_ps=S_ps: e.matmul(S_ps, lhsT=B_tok[:, g, :], rhs=xdtd[:, hs, :].rearrange("p a b -> p (a b)"),
                     start=True, stop=True), reads=[R_Btok, R_xdtd], writes=[BANK[7]], inc=True)

        def stage2b(g):
            hs = slice(4 * g, 4 * g + 4)
            if not samp:
                S_ps = pb(7, 256, 256 * (g % 2))
                sTg = sT[:, hs, :]
                S3 = S_ps.rearrange("p (a b) -> p a b", a=4)
                if not has_state:
                    P.op("act", lambda e: e.copy(out=sTg, in_=S3), reads=[BANK[7]], writes=[R_sT[g]])
                else:
                    P.op("dve", lambda e: e.tensor_tensor(out=tmpS[:], in0=sTg, in1=cd[:, hs].unsqueeze(2).to_broadcast([128, 4, 64]), op=ALU.mult),
                         reads=[R_sT[g], R_cd], writes=[R_tmpS])
                    P.op("dve", lambda e: e.tensor_tensor(out=sTg, in0=tmpS[:], in1=S3, op=ALU.add),
                         reads=[R_tmpS, BANK[7]], writes=[R_sT[g]])
                P.op("act", lambda e: e.copy(out=sT_bf[:, hs, :], in_=sTg), reads=[R_sT[g]], writes=[R_sTbf[g]])

        s1a(0)
        s1a(1)
        s1b_pair(0, 1)
        s1a(2)
        s1a(3)
        if next_prologue is not None:
            next_prologue()
        stage2(0)
        stage2(1)
        s1b_pair(2, 3)
        stage2b(0)
        stage2b(1)
        stage2(2)
        stage2(3)
        stage2b(2)
        stage2b(3)
        if samp:
            ssd_sample_state(l, y_ps)
        P.op("dve", lambda e: e.tensor_tensor(out=yg[:], in0=y_ps, in1=sz[:, :, cols], op=ALU.mult), reads=[BANK[2], BANK[3], R_sz], writes=[R_yg])
        P.op("act", lambda e: e.activation(out=y_sq[:], in_=yg[:], func=AF.Square), reads=[R_yg], writes=[R_ysq])
        gn_ps = pb(6).rearrange("p (g f) -> p g f", g=4)
        for g in range(4):
            for q in range(2):
                P.op("pe", lambda e, g=g, q=q: e.matmul(gn_ps[:, g, :], lhsT=ones_b[:], rhs=y_sq[:, 2 * g + q, :], start=(q == 0), stop=(q == 1)),
                     reads=[R_ysq, R_cb], writes=[BANK[6]], inc=(g == 3 and q == 1))
        P.op("act", lambda e: e.activation(out=g_rstd[:].rearrange("p a b -> p (a b)"), in_=pb(6), func=AF.Ln, bias=EPS, scale=1.0 / 256),
             reads=[BANK[6]], writes=[R_grstd])
        P.op("act", lambda e: e.activation(out=g_rstd[:].rearrange("p a b -> p (a b)"), in_=g_rstd[:].rearrange("p a b -> p (a b)"), func=AF.Exp, scale=-0.5),
             reads=[R_grstd], writes=[R_grstd])
        P.op("dve", lambda e: e.tensor_tensor(out=yg[:], in0=yg[:], in1=vecsT[:, :, V_SSDN(l):V_SSDN(l) + 1].to_broadcast([128, 8, 128]), op=ALU.mult),
             reads=[R_yg, R_tab], writes=[R_yg])

        def evac_tail():
            P.op("dve", lambda e: e.tensor_tensor(out=ynT[:, :, cols].rearrange("p (g q) c -> p g q c", q=2), in0=yg[:].rearrange("p (g q) c -> p g q c", q=2),
                 in1=g_rstd[:].unsqueeze(2).to_broadcast([128, 4, 2, 128]), op=ALU.mult), reads=[R_yg, R_grstd], writes=[R_yn])
        pend_evac.append(evac_tail)
        if (not samp) and gchunk == LP // 128 - 1:
            pend_evac.pop()()
            tr_ps = psum[:, 0:1024].rearrange("p (j f) -> p j f", j=8)
            for j in range(8):
                P.op("pe", lambda e, j=j: e.matmul(tr_ps[:, j, :], lhsT=sT[:, 2 * j:2 * j + 2, :].rearrange("p a b -> p (a b)"), rhs=ident_f, start=True, stop=True),
                     reads=[R_sT[j // 2], R_const], writes=[BANK[j // 4]], inc=(j % 4 == 3))
            P.op("act", lambda e: e.copy(out=yg[:], in_=tr_ps), reads=[BANK[0], BANK[1]], writes=[R_yg])
            P.op("sp", lambda e: e.dma_start(out=ssm_p_out[l].rearrange("(j p) n -> p j n", p=128), in_=yg[:]),
                 reads=[R_yg], writes=[R_out], dma_sem=osem["yg"])

    def ssd_sample_state(l, y_ps):
        P.op("dve", lambda e: e.tensor_copy(out=da_exp[:], in_=da_tok[:].unsqueeze(2).to_broadcast([128, 16, 64])), reads=[R_datok], writes=[R_daexp, R_rc[0], R_r2[0]])
        cdA_ps = pb(5, 128, 320).rearrange("p (j b) -> p j b", j=8)
        for j in range(8):
            P.op("pe", lambda e, j=j: e.matmul(cdA_ps[:, j, :], lhsT=da_exp[:, 2 * j:2 * j + 2, :].rearrange("p a b -> p (a b)"), rhs=ind_f, start=True, stop=True),
                 reads=[R_daexp, R_rc[0], R_r2[0], R_const], writes=[BANK[5]], inc=(j == 7))
        P.op("act", lambda e: e.activation(out=cdA[:], in_=cdA_ps, func=AF.Exp), reads=[BANK[5]], writes=[R_cdA])
        for b in range(NS):
            q = b % 2
            P.op("pool", lambda e, b=b, q=q: e.dma_start(out=s0T_bf[q][:].rearrange("p a b -> p (a b)"), in_=s0T_in[l, b]), writes=[R_s0T[q]], dma_sem=s0T_sem[q])
            P.op("sp", lambda e, b=b, q=q: e.dma_start(out=s0A[q][:], in_=s0A_in[l, b].rearrange("(j p) n -> p j n", p=128)), writes=[R_s0A[q]], dma_sem=s0A_sem[q])
            for h in range(16):
                j, half = h // 2, h % 2
                yb = 2 + (j // 4)
                P.op("pe", lambda e, h=h, j=j, half=half, b=b, q=q: e.matmul(y_ps[64 * half:64 * half + 64, j, 8 * b:8 * b + 8], lhsT=s0T_bf[q][:, h, :],
                     rhs=Cp_all[:, h, 8 * b:8 * b + 8], start=False, stop=(b == NS - 1), skip_group_check=True),
                     reads=[R_s0T[q], R_Cp[h // 4]], writes=[BANK[yb]], inc=(h == 15))
            P.op("dve", lambda e, b=b, q=q: e.tensor_scalar(out=Bm[q][:], in0=B_tok[:], scalar1=ind_f[:, b:b + 1], scalar2=None, op0=ALU.mult),
                 reads=[R_Btok, R_const], writes=[R_Bm[q]])
            sn_ps = psum[:, 0:1024].rearrange("p (j f) -> p j f", j=8)
            for j in range(8):
                P.op("pe", lambda e, j=j, q=q: e.matmul(sn_ps[:, j, :], lhsT=xdtd[:, 2 * j:2 * j + 2, :].rearrange("p a b -> p (a b)"), rhs=Bm[q][:, j // 2, :], start=True, stop=True),
                     reads=[R_xdtd, R_Bm[q]], writes=[BANK[j // 4]], inc=(j % 4 == 3))
            P.op("dve", lambda e, b=b, q=q: e.tensor_tensor(out=s0A[q][:], in0=s0A[q][:], in1=cdA[:, :, b:b + 1].to_broadcast([128, 8, 128]), op=ALU.mult),
                 reads=[R_s0A[q], R_cdA], writes=[R_s0A[q]])
            P.op("dve", lambda e, q=q: e.tensor_tensor(out=s0A[q][:], in0=s0A[q][:], in1=sn_ps, op=ALU.add), reads=[R_s0A[q], BANK[0], BANK[1]], writes=[R_s0A[q]])
            P.op("sp", lambda e, b=b, q=q: e.dma_start(out=ssm_s_out[l, b].rearrange("(j p) n -> p j n", p=128), in_=s0A[q][:]),
                 reads=[R_s0A[q]], writes=[R_out], dma_sem=osem["s0A%d" % q])

    def final_stage():
        for k in range(5):
            final_tile(k)

    def final_tile(k):
        t0, nt, samp = TILES[k]
        if True:
            b = 7
            for kc in range(8):
                s = kc % 2
                P.op("act", lambda e, kc=kc, s=s: e.activation(out=n_sq[s][:, 0:nt], in_=xT[:, kc, t0:t0 + nt], func=AF.Square),
                     reads=[XR[k]], writes=[R_nsq[s]])
                P.op("pe", lambda e, kc=kc, s=s: e.matmul(pb(b, nt), lhsT=ones_b[:], rhs=n_sq[s][:, 0:nt], start=(kc == 0), stop=(kc == 7)),
                     reads=[R_nsq[s], R_cb], writes=[BANK[b]], inc=True)
            rstd_from_psum(b, nt, 1.0 / D, n_rstd[:, 0:nt], n_rstd[:, 0:nt], R_nrstd, R_nrstd)
            for kc in range(8):
                P.op("dve", lambda e, kc=kc: e.scalar_tensor_tensor(out=xT[:, kc, t0:t0 + nt], in0=xT[:, kc, t0:t0 + nt],
                     scalar=vecsT[:, kc, V_NFIN:V_NFIN + 1], in1=n_rstd[:, 0:nt], op0=ALU.mult, op1=ALU.mult),
                     reads=[XR[k], R_tab, R_nrstd], writes=[XR[k]])
            dump_x(k)

    ysrc = yT_out.rearrange("(kc p) t -> p kc t", p=128)

    def dump_x(k):
        t0, nt, _ = TILES[k]
        P.op("sp", lambda e: e.dma_start(out=ysrc[:, :, t0:t0 + nt], in_=xT[:, :, t0:t0 + nt]), reads=[XR[k]], writes=[R_out], dma_sem=osem["x"])

    stage = [0]

    def done():
        stage[0] += 1
        return stop_after is not None and stage[0] >= stop_after

    def run_all():
        for l in range(2):
            mod_ensure(l, 2)
            for st in SUPER:
                ffn_stage(l, 0, st)
            if done():
                return False
            mod_ensure(l, 5)
            handoff(FFN_RES, MIX_RES)
            for k in range(5):
                mix_stage(l, k)
            if done():
                return False
            handoff(MIX_RES, FFN_RES)
            mod_ensure(l, 8)
            for st in SUPER:
                ffn_stage(l, 1, st)
            if done():
                return False
        return True

    if run_all():
        final_stage()
    else:
        for k in range(5):
            dump_x(k)
    for sm in osem.values():
        if sm.val > 0:
            P.q["sp"].append(("w", sm, sm.val))

    with nc.Block() as block:
        @block.tensor
        def _(e):
            P.replay("pe", e)

        @block.scalar
        def _(e):
            P.replay("act", e)

        @block.vector
        def _(e):
            P.replay("dve", e)

        @block.gpsimd
        def _(e):
            P.replay("pool", e)

        @block.sync
        def _(e):
            P.replay("sp", e)
    stack.close()
    return nc, P


def make_consts():
    c = np.zeros((128, 10, 128), np.float32)
    s = np.arange(128)[:, None]
    l = np.arange(128)[None, :]
    c[:, 0, :] = np.eye(128, dtype=np.float32)
    c[:, 1, :] = 1.0
    NEG = -30000.0
    c[:, 2, :] = (s <= l)
    c[:, 3, :] = np.where(l >= s, 0.0, NEG)
    c[:, 4, :] = (l >= s)
    same = (s // LS) == (l // LS)
    c[:, 5, :] = same & (s <= l)
    c[:, 6, :] = np.where(same & (l >= s), 0.0, NEG)
    c[:, 7, :] = same & (l >= s)
    c[:, 8, 0:16] = (s // LS) == np.arange(16)[None, :]
    c[:, 9, :] = (l == (s // LS) * LS + LS - 1)
    return c


def prep_inputs(inp):
    f = lambda a: np.ascontiguousarray(np.asarray(a, dtype=np.float32))
    shared = {}
    def blk(w):
        w = np.asarray(w, dtype=np.float32)
        n = w.shape[2]
        return np.ascontiguousarray(w.reshape(2, 8, 128, n // WB, WB).transpose(0, 3, 2, 1, 4)).reshape(2, n // WB, 128, 8 * WB)
    for n in ["w_ada", "ffn1_w_gu", "ffn2_w_gu", "w_out_ssd", "w_out_sc", "w_o"]:
        shared[n] = blk(inp[n])
    win = np.asarray(inp["w_in"], dtype=np.float32)
    shared["w_in"] = blk(np.concatenate([win[:, :, :C_DT_RAW], win[:, :, C_DT_RAW + 16:]], axis=2))
    shared["w_dt"] = f(win[:, :, C_DT_RAW:C_DT_RAW + 16].reshape(2, 8, 128, 16).transpose(0, 2, 1, 3).reshape(2, 128, 128))
    for n in ["ffn1_w_down", "ffn2_w_down"]:
        w = np.asarray(inp[n], dtype=np.float32)
        shared[n] = np.ascontiguousarray(w.reshape(2, NJ, 128, 8, 128).transpose(0, 3, 2, 1, 4)).reshape(2, 8, 128, NJ * 128)
    shared["badaT_in"] = f(np.asarray(inp["b_ada"]).reshape(2, 72, 128).transpose(2, 0, 1))
    vecs = np.zeros((NV, D), np.float32)
    for l in range(2):
        vecs[V_NF1(l)] = inp["norm_ffn1"][l]
        vecs[V_NMIX(l)] = inp["norm_mix"][l]
        vecs[V_NF2(l)] = inp["norm_ffn2"][l]
        vecs[V_SSDN(l)] = inp["ssd_norm"][l]
        vecs[V_DP(l)] = np.repeat(np.asarray(inp["ssd_d"][l]), 64)
        for k in range(3):
            vecs[V_SCW(l, k)] = inp["sc_conv_w"][l][k]
    vecs[V_NFIN] = inp["norm_final"]
    shared["vecsT_in"] = f(vecs.reshape(NV, 8, 128).transpose(2, 1, 0))
    xv = np.zeros((NX, 2048), np.float32)
    for l in range(2):
        for k in range(4):
            xv[X_CW(l, k)] = inp["ssd_conv_w"][l][k]
        xv[X_CB(l)] = inp["ssd_conv_b"][l]
    shared["xbcv_in"] = f(xv.reshape(NX, 16, 128).transpose(2, 1, 0))
    shared["dtb_in"] = f(np.asarray(inp["ssd_dt_bias"]).T)
    shared["alog_in"] = f(np.broadcast_to(np.asarray(inp["ssd_a_log"]).reshape(1, 32), (128, 32)))
    shared["consts_in"] = make_consts()
    maps = []
    for i in range(NCORES):
        m = dict(shared)
        xs = np.asarray(inp["x_sample"][NS * i:NS * (i + 1)]).reshape(NS * LS, D)
        m["xT_in"] = f(np.concatenate([np.asarray(inp["x_prompt"][i]), xs], axis=0).T)
        call = np.concatenate([np.asarray(inp["c_prompt"][i:i + 1]), np.asarray(inp["c_sample"][NS * i:NS * (i + 1)])], axis=0)
        m["cT_in"] = f(call.reshape(17, 8, 128).transpose(2, 1, 0))
        st = np.asarray(inp["state_ssm"][:, NS * i:NS * (i + 1)]).reshape(2, NS, 1024, 128)
        m["s0A_in"] = f(st)
        m["s0T_in"] = f(st.transpose(0, 1, 3, 2))
        cx = np.asarray(inp["state_conv_ssd"][:, NS * i:NS * (i + 1)])
        m["csx_in"] = f(cx.reshape(2, NS, 3, 16, 128).transpose(0, 4, 3, 1, 2))
        cs = np.asarray(inp["state_conv_short"][:, NS * i:NS * (i + 1)])
        m["css_in"] = f(cs.reshape(2, NS, 2, 8, 128).transpose(0, 4, 3, 1, 2))
        maps.append(m)
    return maps


_CACHE = {}


def kernel(**inputs):
    if "nc" not in _CACHE:
        _CACHE["nc"] = build_program()[0]
    nc = _CACHE["nc"]
    maps = prep_inputs(inputs)
    res = run_bass_kernel_spmd(nc, maps, core_ids=list(range(NCORES)))
    R = res.results
    y_p = np.stack([R[i]["yT_out"][:, :LP].T for i in range(NCORES)], axis=0)
    y_s = np.concatenate([R[i]["yT_out"][:, LP:].T.reshape(NS, LS, D) for i in range(NCORES)], axis=0)
    ssm_p = np.stack([R[i]["ssm_p_out"].reshape(2, 16, 64, 128) for i in range(NCORES)], axis=1)
    csx_p = np.stack([R[i]["csx_p_out"] for i in range(NCORES)], axis=1)
    css_p = np.stack([R[i]["css_p_out"] for i in range(NCORES)], axis=1)
    ssm_s = np.concatenate([R[i]["ssm_s_out"].reshape(2, NS, 16, 64, 128) for i in range(NCORES)], axis=1)
    csx_s = np.concatenate([R[i]["csx_s_out"] for i in range(NCORES)], axis=1)
    css_s = np.concatenate([R[i]["css_s_out"] for i in range(NCORES)], axis=1)
    outs = (y_p, y_s, ssm_p, csx_p, css_p, ssm_s, csx_s, css_s)
    return tuple(np.ascontiguousarray(o, dtype=np.float32) for o in outs)
```

```python
import numpy as np
from contextlib import ExitStack
import concourse.bass as bass
import concourse.mybir as mybir
from concourse.bass_utils import run_bass_kernel_spmd

F32 = mybir.dt.float32
BF16 = mybir.dt.bfloat16
AF = mybir.ActivationFunctionType
ALU = mybir.AluOpType

NCORES = 8
D = 1024
DFF = 2816
NJ = DFF // 128
LP = 2048
NS = 16
LS = 8
NTOK = LP + NS * LS
DPROJ = 8208
EPS = 1e-6
WB = 256
NSLOT = 5
C_Z, C_XBC, C_SCB, C_SCC, C_SCH, C_GSSD, C_GSC = 0, 1024, 3072, 4096, 5120, 6144, 7168
C_DT_RAW = 3072

TILES = [(0, 512, False), (512, 512, False), (1024, 512, False), (1536, 512, False), (2048, 128, True)]
SUPER = [[0, 1, 4], [2, 3]]

def V_NF1(l): return 0 + l
def V_NMIX(l): return 2 + l
def V_NF2(l): return 4 + l
def V_SSDN(l): return 6 + l
def V_DP(l): return 8 + l
def V_SCW(l, k): return 10 + 3 * l + k
V_NFIN = 16
NV = 17
def X_CW(l, k): return 5 * l + k
def X_CB(l): return 5 * l + 4
NX = 10


class Sem:
    __slots__ = ("h", "val", "name")

    def __init__(self, h, name):
        self.h = h
        self.val = 0
        self.name = name


class Res:
    __slots__ = ("name", "w", "r", "excl")

    def __init__(self, name, excl=False):
        self.name = name
        self.w = None
        self.r = {}
        self.excl = excl


class Prog:
    ENGS = ("pe", "act", "dve", "pool", "sp")

    def __init__(self, nc, stack):
        self.nc = nc
        self.stack = stack
        self.q = {e: [] for e in self.ENGS}
        self.esem = {e: self.new_sem("e_" + e) for e in self.ENGS}
        self.waited = {e: {} for e in self.ENGS}
        self.nops = 0

    def new_sem(self, name):
        h = self.stack.enter_context(self.nc.semaphore(name))
        return Sem(h, name)

    def op(self, eng, fn, reads=(), writes=(), inc=True, dma_sem=None):
        need = {}
        ex = [r for r in reads if r.excl]
        if ex:
            writes = list(writes) + ex
            reads = [r for r in reads if not r.excl]

        def add(st):
            if st is None:
                return
            s, v = st
            if need.get(s, 0) < v:
                need[s] = v

        for r in reads:
            add(r.w)
        for w in writes:
            add(w.w)
            for s, v in w.r.items():
                add((s, v))
        mysem = self.esem[eng]
        wd = self.waited[eng]
        for s, v in need.items():
            if s is mysem:
                if eng == "pe" or v > s.val:
                    continue
            if wd.get(s, 0) >= v:
                continue
            wd[s] = v
            self.q[eng].append(("w", s, v))
        if dma_sem is not None:
            dma_sem.val += 16
            st = (dma_sem, dma_sem.val)
            self.q[eng].append(("o", fn, dma_sem, 16))
        elif inc:
            mysem.val += 1
            st = (mysem, mysem.val)
            self.q[eng].append(("o", fn, mysem, 1))
        else:
            st = (mysem, mysem.val + 1)
            self.q[eng].append(("o", fn, None, 0))
        for r in reads:
            s, v = st
            if r.r.get(s, 0) < v:
                r.r[s] = v
        for w in writes:
            w.w = st
            w.r = {}
        self.nops += 1
        return st

    def replay(self, eng, e):
        for it in self.q[eng]:
            if it[0] == "w":
                e.wait_ge(it[1].h, it[2])
            else:
                ins = it[1](e)
                if it[2] is not None:
                    ins.then_inc(it[2].h, it[3])


def handoff(src, dst):
    allst = {}
    for r in src:
        if r.w is not None:
            s, v = r.w
            if allst.get(s, 0) < v:
                allst[s] = v
        for s, v in r.r.items():
            if allst.get(s, 0) < v:
                allst[s] = v
    for d in dst:
        for s, v in allst.items():
            if d.r.get(s, 0) < v:
                d.r[s] = v


class Arena:
    def __init__(self, nc, base, end):
        self.nc = nc
        self.off = base
        self.end = end
        self.n = 0

    def alloc(self, name, shape, dtype, at=None):
        esz = 4 if dtype == F32 else 2
        nbytes = esz
        for s in shape[1:]:
            nbytes *= s
        if at is None:
            off = (self.off + 63) // 64 * 64
            self.off = off + nbytes
            assert self.off <= self.end, f"SBUF arena overflow at {name}: {self.off} > {self.end}"
        else:
            off = at
            assert off % 32 == 0 and off + nbytes <= self.end, name
        self.n += 1
        return self.nc.alloc_sbuf_tensor_at(f"{name}_{self.n}", list(shape), dtype, offset=off)


def build_program(stop_after=None, dbg=False):
    nc = bass.Bass("TRN2", target_bir_lowering=False)
    stack = ExitStack()
    P = Prog(nc, stack)
    A = Arena(nc, 16896, 229376)

    def din(name, shape):
        return nc.dram_tensor(name, list(shape), F32, kind="ExternalInput").ap()

    def dout(name, shape):
        return nc.dram_tensor(name, list(shape), F32, kind="ExternalOutput").ap()

    xT_in = din("xT_in", [D, NTOK])
    cT_in = din("cT_in", [128, 8, 17])
    s0A_in = din("s0A_in", [2, NS, 1024, 128])
    s0T_in = din("s0T_in", [2, NS, 128, 1024])
    csx_in = din("csx_in", [2, 128, 16, NS, 3])
    css_in = din("css_in", [2, 128, 8, NS, 2])
    w_ada = din("w_ada", [2, 9 * D // WB, 128, 8 * WB])
    badaT_in = din("badaT_in", [128, 2, 72])
    vecsT_in = din("vecsT_in", [128, 8, NV])
    xbcv_in = din("xbcv_in", [128, 16, NX])
    dtb_in = din("dtb_in", [16, 2])
    alog_in = din("alog_in", [128, 32])
    consts_in = din("consts_in", [128, 10, 128])
    w_gu = [din("ffn1_w_gu", [2, 2 * DFF // WB, 128, 8 * WB]), din("ffn2_w_gu", [2, 2 * DFF // WB, 128, 8 * WB])]
    w_dn = [din("ffn1_w_down", [2, 8, 128, NJ * 128]), din("ffn2_w_down", [2, 8, 128, NJ * 128])]
    w_in = din("w_in", [2, (DPROJ - 16) // WB, 128, 8 * WB])
    w_dt = din("w_dt", [2, 128, 128])
    w_out_ssd = din("w_out_ssd", [2, D // WB, 128, 8 * WB])
    w_out_sc = din("w_out_sc", [2, D // WB, 128, 8 * WB])
    w_o = din("w_o", [2, D // WB, 128, 8 * WB])

    yT_out = dout("yT_out", [D, NTOK])
    ssm_p_out = dout("ssm_p_out", [2, 1024, 128])
    csx_p_out = dout("csx_p_out", [2, 3, 2048])
    css_p_out = dout("css_p_out", [2, 2, 1024])
    ssm_s_out = dout("ssm_s_out", [2, NS, 1024, 128])
    csx_s_out = dout("csx_s_out", [2, NS, 3, 2048])
    css_s_out = dout("css_s_out", [2, NS, 2, 1024])

    psum = nc.alloc_psum_tensor("psum", [128, 4096], F32)
    BANK = [Res(f"bank{b}", excl=True) for b in range(8)]

    def pb(b, n=512, off=0):
        return psum[:, b * 512 + off:b * 512 + off + n]

    def pb16(b, n=1024, off=0):
        return psum[:, b * 512:(b + 1) * 512].bitcast(BF16)[:, off:off + n]

    xT = A.alloc("xT", [128, 8, NTOK], F32)
    XR = [Res(f"x{k}") for k in range(5)]
    wslot = [A.alloc(f"ws{i}", [128, 8 * WB], BF16) for i in range(NSLOT)]
    WR = [Res(f"ws{i}") for i in range(NSLOT)]
    wsem = [P.new_sem(f"wsem{i}") for i in range(NSLOT)]
    wctr = [0]
    consts = A.alloc("consts", [128, 10, 128], F32)
    R_const = Res("consts")
    ident_f = consts[:, 0, :]
    ones_f = consts[:, 1, :]
    U_f = [consts[:, 2, :], consts[:, 5, :]]
    MB_f = [consts[:, 3, :], consts[:, 6, :]]
    M01_f = [consts[:, 4, :], consts[:, 7, :]]
    ind_f = consts[:, 8, 0:16]
    sel_f = consts[:, 9, :]
    ident_b = A.alloc("ident_b", [128, 128], BF16)
    ones_b = A.alloc("ones_b", [128, 128], BF16)
    mhalf = A.alloc("mhalf", [128, 1], F32)
    R_cb = Res("const_b")
    vecsT = A.alloc("vecsT", [128, 8, NV], F32)
    xbcv = A.alloc("xbcv", [128, 16, NX], F32)
    dtb = A.alloc("dtb", [16, 2], F32)
    a_bc = A.alloc("a_bc", [128, 32], F32)
    cT = A.alloc("cT", [128, 8, 17], F32)
    scT = A.alloc("scT", [128, 8, 17], BF16)
    badaT = A.alloc("badaT", [128, 2, 72], F32)
    R_tab = Res("tables")
    R_abc = Res("a_bc")
    R_scT = Res("scT")
    modT = [A.alloc(f"modT{l}", [128, 72, 17], F32) for l in range(2)]
    R_modt = [[Res(f"mod{l}_{t}") for t in range(9)] for l in range(2)]
    hist_x = A.alloc("hist_x", [128, 16, 3], F32)
    hist_s = A.alloc("hist_s", [128, 8, 2], F32)
    R_hx = Res("hist_x")
    R_hs = Res("hist_s")
    n_sq = [A.alloc(f"n_sq{i}", [128, 512], BF16) for i in range(2)]
    R_nsq = [Res("nsq0"), Res("nsq1")]
    n_tmp = [A.alloc(f"n_tmp{i}", [128, 512], F32) for i in range(2)]
    R_ntmp = [Res("ntmp0"), Res("ntmp1")]
    n_rstd = A.alloc("n_rstd", [128, 512], F32)
    R_nv = Res("n_v")
    R_nrstd = Res("n_rstd")

    OV = A.off
    OV = (OV + 63) // 64 * 64

    A.off = OV
    hT_st = A.alloc("hT_st", [128, 8, 1152], BF16)
    hid = A.alloc("hid", [128, NJ, 1152], BF16)
    wdq = [A.alloc(f"wdq{i}", [128, NJ, 128], BF16) for i in range(2)]
    f_sg = [A.alloc(f"f_sg{i}", [128, 512], F32) for i in range(2)]
    FFN_END = A.off
    R_hTst = [Res(f"hTst{i}") for i in range(3)]
    R_hid = [Res(f"hid{i}") for i in range(3)]
    R_wdq = [Res("wdq0"), Res("wdq1")]
    wdsem = [P.new_sem("wdsem0"), P.new_sem("wdsem1")]
    R_fsg = [Res("fsg0"), Res("fsg1")]
    FFN_RES = R_hTst + R_hid + R_wdq + R_fsg

    A.off = OV
    hT = A.alloc("hT", [128, 8, 512], BF16)
    xbc_c = A.alloc("xbc_c", [128, 16, 512], BF16)
    sz = A.alloc("sz", [128, 8, 512], BF16)
    ynT = sz
    csout = ynT
    dtT = A.alloc("dtT", [16, 512], F32)
    R_hT = Res("hT")
    R_xc = [Res(f"xc{m}") for m in range(16)]
    R_sz = Res("sz")
    R_yn = R_sz
    R_dt = Res("dtT")
    sT = A.alloc("sT", [128, 16, 64], F32)
    sT_bf = A.alloc("sT_bf", [128, 16, 64], BF16)
    SSD0 = (A.off + 63) // 64 * 64
    dt_tok = A.alloc("dt_tok", [128, 16], F32)
    da_tok = A.alloc("da_tok", [128, 16], F32)
    nacum = A.alloc("nacum", [128, 16], F32)
    B_tok = A.alloc("B_tok", [128, 4, 128], BF16)
    xdt_tok = A.alloc("xdt_tok", [128, 16, 64], BF16)
    xdtd = A.alloc("xdtd", [128, 16, 64], BF16)
    cbm = A.alloc("cbm", [128, 4, 128], BF16)
    RC_OFF = (A.off + 63) // 64 * 64
    rhs_cum = [None, None]
    rhs2 = [None, None]
    rhs_cum[0] = A.alloc("rhs_cum0", [128, 4, 128], F32)
    rhs2[0] = A.alloc("rhs2_0", [128, 4, 128], F32)
    rhs_cum[1] = A.alloc("rhs_cum1", [128, 4, 128], F32)
    rhs2[1] = A.alloc("rhs2_1", [128, 4, 128], F32)
    assert A.off == RC_OFF + 8192
    xbc_pre = [nc.alloc_sbuf_tensor_at(f"xbc_pre{i}", [128, 528], F32, offset=RC_OFF + 4096 * i) for i in range(2)]
    Eoff = [A.alloc(f"Eoff{i}", [128, 4, 128], F32) for i in range(2)]
    Cp_all = A.alloc("Cp_all", [128, 16, 128], BF16)
    E_t = [A.alloc(f"E_t{i}", [128, 4, 128], F32) for i in range(2)]
    MT = [A.alloc(f"MT{i}", [128, 4, 128], BF16) for i in range(2)]
    cd = A.alloc("cd", [128, 16], F32)
    YG_OFF = (A.off + 63) // 64 * 64
    yg = A.alloc("yg", [128, 8, 128], F32)
    assert A.off == YG_OFF + 4096
    dt_t1 = nc.alloc_sbuf_tensor_at("dt_t1", [16, 512], F32, offset=YG_OFF)
    dt_t2 = nc.alloc_sbuf_tensor_at("dt_t2", [16, 512], F32, offset=YG_OFF + 2048)
    y_sq = A.alloc("y_sq", [128, 8, 128], BF16)
    diagD = A.alloc("diagD", [128, 8, 128], BF16)
    g_rstd = A.alloc("g_rstd", [128, 4, 128], F32)
    SSDP0 = (A.off + 63) // 64 * 64
    tmpS = A.alloc("tmpS", [128, 4, 64], F32)
    SET1 = dict(dt_tok=A.alloc("dt_tok1", [128, 16], F32), da_tok=A.alloc("da_tok1", [128, 16], F32), nacum=A.alloc("nacum1", [128, 16], F32),
                B_tok=A.alloc("B_tok1", [128, 4, 128], BF16), xdt_tok=A.alloc("xdt_tok1", [128, 16, 64], BF16), cbm=A.alloc("cbm1", [128, 4, 128], BF16))
    SSDP_END = A.off
    A.off = SSDP0
    s0A = [yg, A.alloc("s0A1", [128, 8, 128], F32)]
    s0T_bf = [A.alloc(f"s0T_bf{i}", [128, 16, 64], BF16) for i in range(2)]
    Bm = [A.alloc(f"Bm{i}", [128, 4, 128], BF16) for i in range(2)]
    da_exp = nc.alloc_sbuf_tensor_at("da_exp_alias", [128, 16, 64], F32, offset=RC_OFF)
    cdA = A.alloc("cdA", [128, 8, 16], F32)
    csx = A.alloc("csx", [128, 16, NS, 3], F32)
    css = A.alloc("css", [128, 8, NS, 2], F32)
    SSD_END = max(A.off, SSDP_END)
    R_dtok = Res("dt_tok"); R_datok = Res("da_tok"); R_nacum = Res("nacum"); R_Btok = Res("B_tok")
    R_xdt = Res("xdt_tok"); R_xdtd = Res("xdtd"); R_cbm = Res("cbm")
    R_rc = [Res("rhs_cum0"), Res("rhs_cum1")]; R_r2 = [Res("rhs2_0"), Res("rhs2_1")]
    R_pre = [Res("pre0"), Res("pre1")]
    R_Eoff = [Res("Eoff0"), Res("Eoff1")]; R_Cp = [Res(f"Cp{g}") for g in range(4)]; R_E = [Res("E0"), Res("E1")]
    R_MT = [Res("MT0"), Res("MT1")]; R_cd = Res("cd")
    R_yg = Res("yg"); R_dt1 = R_yg; R_dt2 = R_yg; R_acc = [Res("acc0"), Res("acc1")]; R_ysq = Res("y_sq"); R_gv = Res("g_v"); R_diagD = Res("diagD"); R_grstd = Res("g_rstd")
    R_sT = [Res(f"sT{g}") for g in range(4)]; R_sTbf = [Res(f"sTbf{g}") for g in range(4)]; R_tmpS = Res("tmpS")
    R_s0A = [R_yg, Res("s0A1")]; R_s0T = [Res("s0T0"), Res("s0T1")]; R_Bm = [Res("Bm0"), Res("Bm1")]; R_daexp = Res("da_exp"); R_cdA = Res("cdA")
    R_csx = Res("csx"); R_css = Res("css")
    s0A_sem = [P.new_sem("s0A_sem0"), P.new_sem("s0A_sem1")]; s0T_sem = [P.new_sem("s0T_sem0"), P.new_sem("s0T_sem1")]; cs_sem = P.new_sem("cs_sem"); cs_sem2 = P.new_sem("cs_sem2")
    SET0 = dict(dt_tok=dt_tok, da_tok=da_tok, nacum=nacum, B_tok=B_tok, xdt_tok=xdt_tok, cbm=cbm)
    RSET = [dict(dt_tok=R_dtok, da_tok=R_datok, nacum=R_nacum, B_tok=R_Btok, xdt_tok=R_xdt, cbm=R_cbm),
            dict(dt_tok=Res("dt_tok1"), da_tok=Res("da_tok1"), nacum=Res("nacum1"), B_tok=Res("B_tok1"), xdt_tok=Res("xdt_tok1"), cbm=Res("cbm1"))]
    BSET = [SET0, SET1]
    SSD_RES = list(RSET[1].values()) + [R_dtok, R_datok, R_nacum, R_Btok, R_xdt, R_xdtd, R_cbm] + R_rc + R_r2 + R_Eoff + R_Cp + R_E + R_MT + [R_cd, R_diagD,
               R_yg, R_ysq, R_gv, R_grstd] + [R_tmpS, R_s0A[1]] + R_s0T + R_Bm + [R_daexp, R_cdA, R_csx, R_css]
    A.off = SSD0
    m1 = A.alloc("m1", [128, 8, 512], BF16)
    vT = A.alloc("vT", [128, 8, 512], BF16)
    ch_pre = [A.alloc(f"ch_pre{i}", [128, 520], F32) for i in range(2)]
    p_sg = [A.alloc(f"p_sg{i}", [128, 512], F32) for i in range(2)]
    p_c = [A.alloc(f"p_c{i}", [128, 512], F32) for i in range(2)]
    p_t = A.alloc("p_t", [128, 512], F32)
    p_u = A.alloc("p_u", [128, 512], F32)
    csc_out = A.alloc("csc_out", [128, 1024], F32)
    POST_END = A.off
    assert POST_END <= max(SSD_END, POST_END)
    R_m1 = [Res(f"m1_{m}") for m in range(8)]
    R_vT = Res("vT")
    R_chp = [Res("chp0"), Res("chp1")]
    R_psg = [Res("psg0"), Res("psg1")]
    R_pc = [Res("pc0"), Res("pc1")]
    R_pt = Res("p_t"); R_pu = Res("p_u"); R_csc = Res("csc_out")
    POST_RES = R_m1 + [R_vT] + R_chp + R_psg + R_pc + [R_pt, R_pu, R_csc]
    MIX_RES = [R_hT] + R_pre + R_xc + [R_sz, R_yn, R_dt] + R_sT + R_sTbf + SSD_RES + POST_RES
    MIX_END = max(SSD_END, POST_END)
    assert max(MIX_END, FFN_END) <= 229376, (MIX_END, FFN_END)
    print("SBUF: overlay base", OV, "FFN_END", FFN_END, "SSD0", SSD0, "SSD_END", SSD_END, "POST_END", POST_END)

    osem = {n: P.new_sem("o_" + n) for n in ("cs", "csc", "yg", "s0A0", "s0A1", "x")}
    setup_sem = P.new_sem("setup_sem")
    R_out = Res("out_dummy")

    def wload(src_ap, view):
        i = wctr[0] % NSLOT
        wctr[0] += 1
        dst = view(wslot[i])
        P.op("pool", lambda e, d=dst, s=src_ap: e.dma_start(out=d, in_=s), writes=[WR[i]], dma_sem=wsem[i])
        return wslot[i], WR[i]

    def wblk(wap, l, c0, ncols):
        assert ncols == WB and c0 % WB == 0
        src = wap[l, c0 // WB]
        t, r = wload(src, lambda s: s[:, 0:8 * ncols])
        return t[:, 0:8 * ncols].rearrange("p (kc n) -> p kc n", kc=8), r

    def mm_group(out_ap, pairs, out_res, in_res, last_inc=True, start=True, stop=True):
        n = len(pairs)
        for i, (l_, r_) in enumerate(pairs):
            P.op("pe", lambda e, o=out_ap, a=l_, b=r_, st=(start and i == 0), sp=(stop and i == n - 1):
                 e.matmul(o, lhsT=a, rhs=b, start=st, stop=sp),
                 reads=in_res, writes=[out_res], inc=(last_inc and i == n - 1))

    pbctr = {"a": 0, "b": 0, "c": 0}

    def rot(key, banks):
        b = banks[pbctr[key] % len(banks)]
        pbctr[key] += 1
        return b

    setup_res = []

    def sload(dst, src, res):
        setup_sem.val += 16
        P.q["sp"].append(("o", lambda e, d=dst, s=src: e.dma_start(out=d, in_=s), setup_sem, 16))
        setup_res.append(res)

    sload(consts[:], consts_in, R_const)
    sload(vecsT[:], vecsT_in, R_tab)
    sload(xbcv[:], xbcv_in, R_tab)
    sload(dtb[:], dtb_in, R_tab)
    sload(a_bc[:], alog_in, R_abc)
    sload(cT[:], cT_in, R_tab)
    sload(badaT[:], badaT_in, R_tab)
    xsrc = xT_in.rearrange("(kc p) t -> p kc t", p=128)
    for r in setup_res:
        r.w = (setup_sem, setup_sem.val)
    xsem = [P.new_sem(f"xsem{k}") for k in range(5)]

    def xload(k):
        t0, nt, _ = TILES[k]
        P.op("sp", lambda e: e.dma_start(out=xT[:, :, t0:t0 + nt], in_=xsrc[:, :, t0:t0 + nt]), writes=[XR[k]], dma_sem=xsem[k])
    for k in (0, 1, 4, 2, 3):
        xload(k)

    P.op("dve", lambda e: e.tensor_copy(out=ident_b[:], in_=ident_f), reads=[R_const], writes=[R_cb])
    P.op("dve", lambda e: e.tensor_copy(out=ones_b[:], in_=ones_f), reads=[R_const], writes=[R_cb])
    P.op("dve", lambda e: e.memset(mhalf[:], -0.5), writes=[R_cb])
    P.op("act", lambda e: e.activation(out=a_bc[:], in_=a_bc[:], func=AF.Exp), reads=[R_abc], writes=[R_abc])
    P.op("dve", lambda e: e.tensor_scalar(out=a_bc[:], in0=a_bc[:], scalar1=-1.0, scalar2=None, op0=ALU.mult),
         reads=[R_abc], writes=[R_abc])
    P.op("act", lambda e: e.activation(out=scT[:], in_=cT[:], func=AF.Silu), reads=[R_tab], writes=[R_scT])

    nbt = D // WB
    gains = {1: (lambda l: V_NF1(l)), 4: (lambda l: V_NMIX(l)), 7: (lambda l: V_NF2(l))}

    def mod_block(l, cb_):
        t = cb_ // nbt
        wt, wr = wblk(w_ada, l, cb_ * WB, WB)
        for cc in range(WB // 128):
            q = cb_ * (WB // 128) + cc
            b = 7
            mm_group(pb(b, 17), [(wt[:, kc, cc * 128:(cc + 1) * 128], scT[:, kc, :]) for kc in range(8)],
                     BANK[b], [wr, R_scT])
            P.op("dve", lambda e, q=q: e.tensor_scalar(out=modT[l][:, q, :], in0=pb(b, 17),
                 scalar1=badaT[:, l, q:q + 1], scalar2=None, op0=ALU.add),
                 reads=[BANK[b], R_tab], writes=[R_modt[l][t]])
        if cb_ % nbt == nbt - 1:
            v = modT[l][:, t * 8:(t + 1) * 8, :]
            if t in gains:
                vi = gains[t](l)
                P.op("dve", lambda e: e.tensor_scalar(out=v, in0=v, scalar1=1.0, scalar2=None, op0=ALU.add),
                     reads=[R_modt[l][t]], writes=[R_modt[l][t]])
                P.op("dve", lambda e: e.tensor_tensor(out=v, in0=v, in1=vecsT[:, :, vi:vi + 1].to_broadcast([128, 8, 17]), op=ALU.mult),
                     reads=[R_modt[l][t], R_tab], writes=[R_modt[l][t]])
            elif t in (2, 8):
                P.op("dve", lambda e: e.tensor_scalar(out=v, in0=v, scalar1=0.5, scalar2=None, op0=ALU.mult),
                     reads=[R_modt[l][t]], writes=[R_modt[l][t]])

    mod_q = [(l, cb_) for l in range(2) for cb_ in range(9 * nbt)]
    mod_done = [0]

    def sprinkle(n=1):
        for _ in range(n):
            if mod_done[0] < len(mod_q):
                mod_block(*mod_q[mod_done[0]])
                mod_done[0] += 1

    def mod_ensure(l, t):
        need = (l * 9 + t + 1) * nbt
        while mod_done[0] < need:
            sprinkle()

    mod_ensure(0, 2)

    def modA(l, i):
        return modT[l][:, (3 * i + 1) * 8:(3 * i + 2) * 8, :]

    def modB(l, i):
        return modT[l][:, (3 * i) * 8:(3 * i + 1) * 8, :]

    def modG(l, i):
        return modT[l][:, (3 * i + 2) * 8:(3 * i + 3) * 8, :]

    def rstd_from_psum(b, nt, scale, v_ap, rstd_ap, r_v, r_rstd):
        P.op("act", lambda e: e.activation(out=v_ap, in_=pb(b, nt), func=AF.Ln, bias=EPS, scale=scale),
             reads=[BANK[b]], writes=[r_v])
        P.op("act", lambda e: e.activation(out=rstd_ap, in_=v_ap, func=AF.Exp, scale=-0.5),
             reads=[r_v], writes=[r_rstd])

    def norm_mod(l, i, k, hdst, r_h):
        t0, nt, samp = TILES[k]
        b = 7
        for kc in range(8):
            s = kc % 2
            P.op("act", lambda e, kc=kc, s=s: e.activation(out=n_sq[s][:, 0:nt], in_=xT[:, kc, t0:t0 + nt], func=AF.Square),
                 reads=[XR[k]], writes=[R_nsq[s]])
            P.op("pe", lambda e, kc=kc, s=s: e.matmul(pb(b, nt), lhsT=ones_b[:], rhs=n_sq[s][:, 0:nt], start=(kc == 0), stop=(kc == 7)),
                 reads=[R_nsq[s], R_cb], writes=[BANK[b]], inc=True)
        rstd_from_psum(b, nt, 1.0 / D, n_rstd[:, 0:nt], n_rstd[:, 0:nt], R_nrstd, R_nrstd)
        Am, Bm_ = modA(l, i), modB(l, i)
        for kc in range(8):
            s = kc % 2
            P.op("dve", lambda e, kc=kc, s=s: e.tensor_tensor(out=n_tmp[s][:, 0:nt], in0=xT[:, kc, t0:t0 + nt], in1=n_rstd[:, 0:nt], op=ALU.mult),
                 reads=[XR[k], R_nrstd], writes=[R_ntmp[s]])
            if not samp:
                P.op("act", lambda e, kc=kc, s=s: e.activation(out=hdst[:, kc, :], in_=n_tmp[s][:, 0:nt], func=AF.Identity,
                     bias=Bm_[:, kc, 0:1], scale=Am[:, kc, 0:1]), reads=[R_ntmp[s], R_modt[l][3 * i], R_modt[l][3 * i + 1]], writes=[r_h])
            else:
                tv = n_tmp[s][:, 0:nt].rearrange("p (b t) -> p b t", t=LS)
                P.op("dve", lambda e, kc=kc, tv=tv: e.tensor_tensor(out=tv, in0=tv,
                     in1=Am[:, kc, 1:17].unsqueeze(2).to_broadcast([128, NS, LS]), op=ALU.mult),
                     reads=[R_ntmp[s], R_modt[l][3 * i + 1]], writes=[R_ntmp[s]])
                P.op("dve", lambda e, kc=kc, tv=tv: e.tensor_tensor(out=hdst[:, kc, :].rearrange("p (b t) -> p b t", t=LS), in0=tv,
                     in1=Bm_[:, kc, 1:17].unsqueeze(2).to_broadcast([128, NS, LS]), op=ALU.add),
                     reads=[R_ntmp[s], R_modt[l][3 * i]], writes=[r_h])

    def resid_update(l, i, k, m, b, nt, t0, samp, tmp_ap, r_tmp):
        G = modG(l, i)
        if not samp:
            P.op("dve", lambda e: e.scalar_tensor_tensor(out=xT[:, m, t0:t0 + nt], in0=pb(b, nt), scalar=G[:, m, 0:1],
                 in1=xT[:, m, t0:t0 + nt], op0=ALU.mult, op1=ALU.add), reads=[BANK[b], R_modt[l][3 * i + 2], XR[k]], writes=[XR[k]])
        else:
            tv = tmp_ap[:, 0:nt].rearrange("p (b t) -> p b t", t=LS)
            P.op("dve", lambda e: e.tensor_tensor(out=tv, in0=pb(b, nt).rearrange("p (b t) -> p b t", t=LS),
                 in1=G[:, m, 1:17].unsqueeze(2).to_broadcast([128, NS, LS]), op=ALU.mult),
                 reads=[BANK[b], R_modt[l][3 * i + 2]], writes=[r_tmp])
            P.op("dve", lambda e: e.tensor_tensor(out=xT[:, m, t0:t0 + nt], in0=xT[:, m, t0:t0 + nt], in1=tmp_ap[:, 0:nt], op=ALU.add),
                 reads=[r_tmp, XR[k]], writes=[XR[k]])

    def ffn_stage(l, which, tiles_):
        i = 0 if which == 0 else 2
        offs = []
        o = 0
        for k in tiles_:
            offs.append(o)
            o += TILES[k][1]
        for si, k in enumerate(tiles_):
            nt = TILES[k][1]
            norm_mod(l, i, k, hT_st[:, :, offs[si]:offs[si] + nt], R_hTst[si])
        wgu = w_gu[which]
        nblk = DFF // WB
        for jb in range(nblk):
            wg, rg = wblk(wgu, l, jb * WB, WB)
            wu, ru = wblk(wgu, l, DFF + jb * WB, WB)
            for jj in range(WB // 128):
                j = jb * (WB // 128) + jj
                for si, k in enumerate(tiles_):
                    nt = TILES[k][1]
                    o0 = offs[si]
                    bg = rot("a", [0, 1])
                    bu = rot("b", [2, 3])
                    mm_group(pb(bg, nt), [(wg[:, kc, jj * 128:(jj + 1) * 128], hT_st[:, kc, o0:o0 + nt]) for kc in range(8)],
                             BANK[bg], [rg, R_hTst[si]])
                    mm_group(pb(bu, nt), [(wu[:, kc, jj * 128:(jj + 1) * 128], hT_st[:, kc, o0:o0 + nt]) for kc in range(8)],
                             BANK[bu], [ru, R_hTst[si]])
                    s = pbctr["c"] % 2
                    pbctr["c"] += 1
                    P.op("act", lambda e, s=s, bg=bg, nt=nt: e.activation(out=f_sg[s][:, 0:nt], in_=pb(bg, nt), func=AF.Silu),
                         reads=[BANK[bg]], writes=[R_fsg[s]])
                    P.op("dve", lambda e, s=s, bu=bu, nt=nt, j=j, o0=o0: e.tensor_tensor(out=hid[:, j, o0:o0 + nt], in0=f_sg[s][:, 0:nt],
                         in1=pb(bu, nt), op=ALU.mult), reads=[R_fsg[s], BANK[bu]], writes=[R_hid[si]])
            sprinkle(1)
        wdn = w_dn[which]
        for m in range(8):
            s = m % 2
            src = wdn[l, m]
            P.op("pool", lambda e, s=s, src=src: e.dma_start(out=wdq[s][:].rearrange("p j n -> p (j n)"), in_=src), writes=[R_wdq[s]], dma_sem=wdsem[s])
            for si, k in enumerate(tiles_):
                t0, nt, samp = TILES[k]
                o0 = offs[si]
                b = rot("c", [4, 5, 6])
                mm_group(pb(b, nt), [(wdq[s][:, j, :], hid[:, j, o0:o0 + nt]) for j in range(NJ)], BANK[b], [R_wdq[s], R_hid[si]])
                resid_update(l, i, k, m, b, nt, t0, samp, f_sg[0], R_fsg[0])
            sprinkle(1)

    def proj_chunks(wap, l, c0, nchunks, hsrc, r_hsrc, nt, consume):
        nb = (nchunks * 128 + WB - 1) // WB
        for bi in range(nb):
            wt, wr = wblk(wap, l, c0 + bi * WB, WB)
            for cc in range(WB // 128):
                ci = bi * (WB // 128) + cc
                b = rot("a", [0, 1, 2, 3, 4, 5, 6, 7])
                mm_group(pb(b, nt), [(wt[:, kc, cc * 128:(cc + 1) * 128], hsrc[:, kc, 0:nt]) for kc in range(8)],
                         BANK[b], [wr, r_hsrc])
                consume(ci, b, wt, wr, cc)

    def mix_stage(l, k):
        t0, nt, samp = TILES[k]
        first = (k == 0)
        v = 1 if samp else 0
        nchunk = nt // 128
        norm_mod(l, 1, k, hT[:, :, 0:nt], R_hT)
        if samp:
            P.op("sp", lambda e: e.dma_start(out=csx[:], in_=csx_in[l]), writes=[R_csx], dma_sem=cs_sem)
            P.op("sp", lambda e: e.dma_start(out=css[:], in_=css_in[l]), writes=[R_css], dma_sem=cs_sem2)
        if first:
            P.op("dve", lambda e: e.memset(hist_x[:], 0.0), writes=[R_hx])
            P.op("dve", lambda e: e.memset(hist_s[:], 0.0), writes=[R_hs])
        want_cs = (k == 3) or samp
        csv = csout[:].rearrange("p a b -> p (a b)").bitcast(F32)

        handoff(R_rc + R_r2, R_pre)
        handoff([R_yg], R_acc)
        pend_silu = []

        def consume_xbc(m, b, wt, wr, cc):
            s = m % 2
            pre = xbc_pre[s]
            if not samp:
                P.op("act", lambda e: e.copy(out=pre[:, 0:3], in_=hist_x[:, m, :]), reads=[R_hx], writes=[R_pre[s]])
                P.op("act", lambda e: e.copy(out=pre[:, 3:3 + nt], in_=pb(b, nt)), reads=[BANK[b]], writes=[R_pre[s]])
                P.op("act", lambda e: e.copy(out=hist_x[:, m, :], in_=pre[:, nt:nt + 3]), reads=[R_pre[s]], writes=[R_hx])
                ext = lambda kk: pre[:, kk:kk + nt]
                acc = pre[:, 0:nt]
                accv = acc
            else:
                pv = pre[:, 0:NS * 11].rearrange("p (b t) -> p b t", t=11)
                P.op("act", lambda e: e.copy(out=pv[:, :, 0:3], in_=csx[:, m, :, :]), reads=[R_csx], writes=[R_pre[s]])
                P.op("act", lambda e: e.copy(out=pv[:, :, 3:11], in_=pb(b, nt).rearrange("p (b t) -> p b t", t=LS)),
                     reads=[BANK[b]], writes=[R_pre[s]])
                ext = lambda kk: pv[:, :, kk:kk + LS]
            cacc = yg[:].rearrange("p a b -> p (a b)")[:, 512 * s:512 * s + nt]
            caccv = cacc if not samp else cacc.rearrange("p (b t) -> p b t", t=LS)
            P.op("act", lambda e: e.activation(out=cacc, in_=pb(b, nt), func=AF.Identity, scale=xbcv[:, m, X_CW(l, 3):X_CW(l, 3) + 1]),
                 reads=[BANK[b], R_tab], writes=[R_acc[s]])
            for kk in range(3):
                P.op("dve", lambda e, kk=kk: e.scalar_tensor_tensor(out=caccv, in0=ext(kk), scalar=xbcv[:, m, X_CW(l, kk):X_CW(l, kk) + 1],
                     in1=caccv, op0=ALU.mult, op1=ALU.add), reads=[R_pre[s], R_tab, R_acc[s]], writes=[R_acc[s]])
            def silu_m():
                P.op("act", lambda e: e.activation(out=xbc_c[:, m, 0:nt], in_=cacc, func=AF.Silu, bias=xbcv[:, m, X_CB(l):X_CB(l) + 1]),
                     reads=[R_acc[s], R_tab], writes=[R_xc[m]])
            if pend_silu:
                pend_silu.pop()()
            pend_silu.append(silu_m)
            if want_cs and cc == WB // 128 - 1:
                c0 = (m // (WB // 128)) * WB
                bb = rot("a", [0, 1, 2, 3, 4, 5, 6, 7])
                if samp:
                    mtok, tsl = 128, slice(0, 128)
                else:
                    mtok, tsl = 3, slice(nt - 3, nt)
                mm_group(psum[0:mtok, bb * 512:bb * 512 + WB], [(hT[:, kc, tsl], wt[:, kc, :]) for kc in range(8)], BANK[bb], [wr, R_hT])
                P.op("act", lambda e: e.copy(out=csv[0:mtok, c0:c0 + WB], in_=psum[0:mtok, bb * 512:bb * 512 + WB]),
                     reads=[BANK[bb]], writes=[R_yn])

        proj_chunks(w_in, l, C_XBC, 16, hT, R_hT, nt, consume_xbc)
        pend_silu.pop()()
        handoff(R_acc, [R_yg])
        if want_cs:
            if not samp:
                P.op("sp", lambda e: e.dma_start(out=csx_p_out[l], in_=csv[0:3, :]), reads=[R_yn], writes=[R_out], dma_sem=osem["cs"])
            else:
                for r in range(3):
                    P.op("sp", lambda e, r=r: e.dma_start(out=csx_s_out[l, :, r, :], in_=csv[5 + r:128:8, :]),
                         reads=[R_yn], writes=[R_out], dma_sem=osem["cs"])

        wt_, wr_ = wload(w_dt[l], lambda s_: s_[:, 0:128])
        wdt = wt_[:, 0:128].rearrange("p (kc n) -> p kc n", kc=8)
        b = rot("a", [0, 1, 2, 3, 4, 5, 6, 7])
        mm_group(psum[0:16, b * 512:b * 512 + nt], [(wdt[:, kc, :], hT[:, kc, 0:nt]) for kc in range(8)], BANK[b], [wr_, R_hT])
        P.op("act", lambda e: e.activation(out=dt_t1[:, 0:nt], in_=psum[0:16, b * 512:b * 512 + nt], func=AF.Identity, bias=dtb[:, l:l + 1]),
             reads=[BANK[b], R_tab], writes=[R_dt1])
        P.op("act", lambda e: e.activation(out=dt_t2[:, 0:nt], in_=dt_t1[:, 0:nt], func=AF.Abs), reads=[R_dt1], writes=[R_dt2])
        P.op("act", lambda e: e.activation(out=dt_t2[:, 0:nt], in_=dt_t2[:, 0:nt], func=AF.Exp, scale=-1.0), reads=[R_dt2], writes=[R_dt2])
        P.op("act", lambda e: e.activation(out=dt_t2[:, 0:nt], in_=dt_t2[:, 0:nt], func=AF.Ln, bias=1.0), reads=[R_dt2], writes=[R_dt2])
        P.op("dve", lambda e: e.tensor_scalar(out=dt_t1[:, 0:nt], in0=dt_t1[:, 0:nt], scalar1=0.0, scalar2=None, op0=ALU.max),
             reads=[R_dt1], writes=[R_dt1])
        P.op("dve", lambda e: e.tensor_tensor(out=dtT[:, 0:nt], in0=dt_t1[:, 0:nt], in1=dt_t2[:, 0:nt], op=ALU.add),
             reads=[R_dt1, R_dt2], writes=[R_dt])

        def consume_z(m, b, wt, wr, cc):
            P.op("act", lambda e: e.activation(out=sz[:, m, 0:nt], in_=pb(b, nt), func=AF.Silu), reads=[BANK[b]], writes=[R_sz])
        proj_chunks(w_in, l, C_Z, 8, hT, R_hT, nt, consume_z)

        handoff(R_pre, R_rc + R_r2)
        for j in range(8):
            P.op("dve", lambda e, j=j: e.tensor_scalar(out=diagD[:, j, :], in0=ident_f, scalar1=vecsT[:, j, V_DP(l):V_DP(l) + 1], scalar2=None, op0=ALU.mult),
                 reads=[R_const, R_tab], writes=[R_diagD])
        ssd_prologue(l, k, 0, samp, v)
        for ci in range(nchunk):
            nxt = (lambda c=ci + 1: ssd_prologue(l, k, c, samp, v)) if ci + 1 < nchunk else None
            ssd_chunk(l, k, ci, samp, v, nxt)
        if pend_evac:
            pend_evac.pop()()

        handoff(SSD_RES, POST_RES)

        gate_banks = {}

        def consume_gate(m, b, wt, wr, cc):
            s = m % 2
            P.op("act", lambda e: e.activation(out=p_sg[s][:, 0:nt], in_=pb(b, nt), func=AF.Sigmoid), reads=[BANK[b]], writes=[R_psg[s]])

        def consume_yssd(m, b, wt, wr, cc):
            s = m % 2
            P.op("dve", lambda e: e.tensor_tensor(out=m1[:, m, 0:nt], in0=p_sg[s][:, 0:nt], in1=pb(b, nt), op=ALU.mult),
                 reads=[R_psg[s], BANK[b]], writes=[R_m1[m]])

        for half in range(8 // (WB // 128)):
            nn = WB // 128
            proj_sub(w_in, l, C_GSSD + half * WB, hT, R_hT, nt, lambda ci, b, wt, wr, cc, h_=half: consume_gate(h_ * nn + ci, b, wt, wr, cc))
            proj_sub(w_out_ssd, l, half * WB, ynT, R_yn, nt, lambda ci, b, wt, wr, cc, h_=half: consume_yssd(h_ * nn + ci, b, wt, wr, cc))

        want_cs2 = want_cs
        for half in range(8 // (WB // 128)):
            nn = WB // 128

            def consume_c(ci, b, wt, wr, cc, h_=half):
                s = (h_ * nn + ci) % 2
                P.op("act", lambda e: e.copy(out=p_c[s][:, 0:nt], in_=pb(b, nt)), reads=[BANK[b]], writes=[R_pc[s]])
                if want_cs2 and cc == nn - 1:
                    tok_mm(wt, wr, "c", h_)

            def consume_h(ci, b, wt, wr, cc, h_=half):
                m = h_ * nn + ci
                s = m % 2
                pre = ch_pre[s]
                if not samp:
                    P.op("act", lambda e: e.copy(out=pre[:, 0:2], in_=hist_s[:, m, :]), reads=[R_hs], writes=[R_chp[s]])
                    P.op("dve", lambda e: e.tensor_tensor(out=pre[:, 2:2 + nt], in0=p_c[s][:, 0:nt], in1=pb(b, nt), op=ALU.mult),
                         reads=[R_pc[s], BANK[b]], writes=[R_chp[s]])
                    P.op("act", lambda e: e.copy(out=hist_s[:, m, :], in_=pre[:, nt:nt + 2]), reads=[R_chp[s]], writes=[R_hs])
                    ext = lambda kk: pre[:, kk:kk + nt]
                    uv = p_u[:, 0:nt]
                else:
                    pv = pre[:, 0:NS * 10].rearrange("p (b t) -> p b t", t=10)
                    P.op("act", lambda e: e.copy(out=pv[:, :, 0:2], in_=css[:, m, :, :]), reads=[R_css], writes=[R_chp[s]])
                    P.op("dve", lambda e: e.tensor_tensor(out=pv[:, :, 2:10], in0=p_c[s][:, 0:nt].rearrange("p (b t) -> p b t", t=LS),
                         in1=pb(b, nt).rearrange("p (b t) -> p b t", t=LS), op=ALU.mult), reads=[R_pc[s], BANK[b]], writes=[R_chp[s]])
                    ext = lambda kk: pv[:, :, kk:kk + LS]
                    uv = p_u[:, 0:nt].rearrange("p (b t) -> p b t", t=LS)
                P.op("dve", lambda e: e.tensor_scalar(out=uv, in0=ext(0), scalar1=vecsT[:, m, V_SCW(l, 0):V_SCW(l, 0) + 1], scalar2=None, op0=ALU.mult),
                     reads=[R_chp[s], R_tab], writes=[R_pu])
                for kk in range(1, 3):
                    P.op("dve", lambda e, kk=kk: e.scalar_tensor_tensor(out=uv, in0=ext(kk), scalar=vecsT[:, m, V_SCW(l, kk):V_SCW(l, kk) + 1],
                         in1=uv, op0=ALU.mult, op1=ALU.add), reads=[R_chp[s], R_tab, R_pu], writes=[R_pu])
                if want_cs2 and cc == nn - 1:
                    tok_mm(wt, wr, "h", h_)

            def consume_b(ci, b, wt, wr, cc, h_=half):
                pass

            def tok_mm(wt, wr, which_, h_):
                bb = rot("a", [0, 1, 2, 3, 4, 5, 6, 7])
                if samp:
                    mtok, tsl = 128, slice(0, 128)
                else:
                    mtok, tsl = 2, slice(nt - 2, nt)
                mm_group(psum[0:mtok, bb * 512:bb * 512 + WB], [(hT[:, kc, tsl], wt[:, kc, :]) for kc in range(8)], BANK[bb], [wr, R_hT])
                dst = csc_out[0:mtok, h_ * WB:(h_ + 1) * WB]
                if which_ == "c":
                    P.op("act", lambda e: e.copy(out=dst, in_=psum[0:mtok, bb * 512:bb * 512 + WB]), reads=[BANK[bb]], writes=[R_csc])
                else:
                    P.op("dve", lambda e: e.tensor_tensor(out=dst, in0=dst, in1=psum[0:mtok, bb * 512:bb * 512 + WB], op=ALU.mult),
                         reads=[BANK[bb], R_csc], writes=[R_csc])

            wc, rc_ = wblk(w_in, l, C_SCC + half * WB, WB)
            wh, rh_ = wblk(w_in, l, C_SCH + half * WB, WB)
            wb_, rb_ = wblk(w_in, l, C_SCB + half * WB, WB)
            for cc in range(nn):
                m = half * nn + cc
                b1 = rot("a", [0, 1, 2, 3, 4, 5, 6, 7])
                mm_group(pb(b1, nt), [(wc[:, kc, cc * 128:(cc + 1) * 128], hT[:, kc, 0:nt]) for kc in range(8)], BANK[b1], [rc_, R_hT])
                consume_c(cc, b1, wc, rc_, cc)
                b2 = rot("a", [0, 1, 2, 3, 4, 5, 6, 7])
                mm_group(pb(b2, nt), [(wh[:, kc, cc * 128:(cc + 1) * 128], hT[:, kc, 0:nt]) for kc in range(8)], BANK[b2], [rh_, R_hT])
                consume_h(cc, b2, wh, rh_, cc)
                b3 = rot("a", [0, 1, 2, 3, 4, 5, 6, 7])
                mm_group(pb(b3, nt), [(wb_[:, kc, cc * 128:(cc + 1) * 128], hT[:, kc, 0:nt]) for kc in range(8)], BANK[b3], [rb_, R_hT])
                P.op("dve", lambda e, m=m, b3=b3: e.tensor_tensor(out=vT[:, m, 0:nt], in0=p_u[:, 0:nt], in1=pb(b3, nt), op=ALU.mult),
                     reads=[R_pu, BANK[b3]], writes=[R_vT])
        if want_cs2:
            if not samp:
                P.op("sp", lambda e: e.dma_start(out=css_p_out[l], in_=csc_out[0:2, :]), reads=[R_csc], writes=[R_out], dma_sem=osem["csc"])
            else:
                for r in range(2):
                    P.op("sp", lambda e, r=r: e.dma_start(out=css_s_out[l, :, r, :], in_=csc_out[6 + r:128:8, :]),
                         reads=[R_csc], writes=[R_out], dma_sem=osem["csc"])

        def consume_gate2(m, b, wt, wr, cc):
            s = m % 2
            P.op("act", lambda e: e.activation(out=p_sg[s][:, 0:nt], in_=pb(b, nt), func=AF.Sigmoid), reads=[BANK[b]], writes=[R_psg[s]])

        def consume_ysc(m, b, wt, wr, cc):
            s = m % 2
            P.op("dve", lambda e: e.tensor_tensor(out=p_t[:, 0:nt], in0=p_sg[s][:, 0:nt], in1=pb(b, nt), op=ALU.mult),
                 reads=[R_psg[s], BANK[b]], writes=[R_pt])
            P.op("dve", lambda e: e.tensor_tensor(out=m1[:, m, 0:nt], in0=m1[:, m, 0:nt], in1=p_t[:, 0:nt], op=ALU.add),
                 reads=[R_pt, R_m1[m]], writes=[R_m1[m]])

        for half in range(8 // (WB // 128)):
            nn = WB // 128
            proj_sub(w_in, l, C_GSC + half * WB, hT, R_hT, nt, lambda ci, b, wt, wr, cc, h_=half: consume_gate2(h_ * nn + ci, b, wt, wr, cc))
            proj_sub(w_out_sc, l, half * WB, vT, R_vT, nt, lambda ci, b, wt, wr, cc, h_=half: consume_ysc(h_ * nn + ci, b, wt, wr, cc))

        def consume_o(m, b, wt, wr, cc):
            resid_update(l, 1, k, m, b, nt, t0, samp, p_t, R_pt)
        proj_chunks_multi(w_o, l, 0, 8, m1, R_m1, nt, consume_o)

        handoff(POST_RES, SSD_RES)

    def proj_sub(wap, l, c0, hsrc, r_hsrc, nt, consume):
        proj_chunks(wap, l, c0, WB // 128, hsrc, r_hsrc, nt, consume)

    def proj_chunks_multi(wap, l, c0, nchunks, hsrc, r_list, nt, consume):
        nb = (nchunks * 128) // WB
        for bi in range(nb):
            wt, wr = wblk(wap, l, c0 + bi * WB, WB)
            for cc in range(WB // 128):
                ci = bi * (WB // 128) + cc
                b = rot("a", [0, 1, 2, 3, 4, 5, 6, 7])
                mm_group(pb(b, nt), [(wt[:, kc, cc * 128:(cc + 1) * 128], hsrc[:, kc, 0:nt]) for kc in range(8)],
                         BANK[b], [wr] + list(r_list))
                consume(ci, b, wt, wr, cc)

    pend_evac = []

    def ssd_prologue(l, k, ci, samp, v):
        t0, nt, _ = TILES[k]
        c0 = ci * 128
        cols = slice(c0, c0 + 128)
        a_l = a_bc[:, 16 * l:16 * l + 16]
        pq = ci % 2
        dt_tok, da_tok, nacum, B_tok, xdt_tok, cbm = (BSET[pq][n] for n in ("dt_tok", "da_tok", "nacum", "B_tok", "xdt_tok", "cbm"))
        R_dtok, R_datok, R_nacum, R_Btok, R_xdt, R_cbm = (RSET[pq][n] for n in ("dt_tok", "da_tok", "nacum", "B_tok", "xdt_tok", "cbm"))
        xtok_ps = pb16(4).rearrange("p (m f) -> p m f", m=8)
        for m in range(8):
            P.op("pe", lambda e, m=m: e.transpose(xtok_ps[:, m, :], xbc_c[:, m, cols], ident_b[:]),
                 reads=[R_xc[m], R_cb], writes=[BANK[4]], inc=(m == 7))
        btok_ps = pb16(5, 512).rearrange("p (g f) -> p g f", g=4)
        for g in range(4):
            P.op("pe", lambda e, g=g: e.transpose(btok_ps[:, g, :], xbc_c[:, 8 + g, cols], ident_b[:]),
                 reads=[R_xc[8 + g], R_cb], writes=[BANK[5]], inc=False)
        dttok_ps = pb(5, 16, 256)
        acum_ps = pb(5, 16, 288)
        P.op("pe", lambda e: e.matmul(dttok_ps, lhsT=dtT[:, cols], rhs=ident_f[0:16, 0:16], start=True, stop=True),
             reads=[R_dt, R_const], writes=[BANK[5]], inc=True)
        P.op("dve", lambda e: e.tensor_copy(out=dt_tok[:], in_=dttok_ps), reads=[BANK[5]], writes=[R_dtok])
        P.op("dve", lambda e: e.tensor_tensor(out=da_tok[:], in0=dt_tok[:], in1=a_l, op=ALU.mult), reads=[R_dtok, R_abc], writes=[R_datok])
        P.op("act", lambda e: e.copy(out=B_tok[:], in_=btok_ps), reads=[BANK[5]], writes=[R_Btok])
        P.op("dve", lambda e: e.tensor_tensor(out=xdt_tok[:], in0=pb16(4).rearrange("p (h f) -> p h f", h=16),
             in1=dt_tok[:].unsqueeze(2).to_broadcast([128, 16, 64]), op=ALU.mult), reads=[BANK[4], R_dtok], writes=[R_xdt])
        P.op("pe", lambda e: e.matmul(acum_ps, lhsT=U_f[v], rhs=da_tok[:], start=True, stop=True),
             reads=[R_datok, R_const], writes=[BANK[5]], inc=True)
        P.op("dve", lambda e: e.tensor_copy(out=nacum[:], in_=acum_ps), reads=[BANK[5]], writes=[R_nacum])
        cb_ps = pb(6).rearrange("p (g f) -> p g f", g=4)
        for g in range(4):
            P.op("pe", lambda e, g=g: e.matmul(cb_ps[:, g, :], lhsT=xbc_c[:, 8 + g, cols], rhs=xbc_c[:, 12 + g, cols], start=True, stop=True),
                 reads=[R_xc[8 + g], R_xc[12 + g]], writes=[BANK[6]], inc=(g == 3))
        P.op("dve", lambda e: e.tensor_tensor(out=cbm[:], in0=cb_ps, in1=M01_f[v].unsqueeze(1).to_broadcast([128, 4, 128]), op=ALU.mult),
             reads=[BANK[6], R_const], writes=[R_cbm])

    def ssd_chunk(l, k, ci, samp, v, next_prologue=None):
        t0, nt, _ = TILES[k]
        c0 = ci * 128
        cols = slice(c0, c0 + 128)
        gchunk = (t0 // 128 + ci) if not samp else 0
        has_state = (not samp) and gchunk > 0
        pq = ci % 2
        dt_tok, da_tok, nacum, B_tok, xdt_tok, cbm = (BSET[pq][n] for n in ("dt_tok", "da_tok", "nacum", "B_tok", "xdt_tok", "cbm"))
        R_dtok, R_datok, R_nacum, R_Btok, R_xdt, R_cbm = (RSET[pq][n] for n in ("dt_tok", "da_tok", "nacum", "B_tok", "xdt_tok", "cbm"))
        if pend_evac:
            pend_evac.pop()()
        y_ps = psum[:, 2 * 512:4 * 512].rearrange("p (j f) -> p j f", j=8)
        if samp:
            P.op("dve", lambda e: e.memset(psum[:, 2 * 512:4 * 512], 0.0), writes=[BANK[2], BANK[3]])
        def s1a(g):
            rb = g % 2
            hs = slice(4 * g, 4 * g + 4)
            P.op("pool", lambda e: e.tensor_tensor(out=rhs_cum[rb][:], in0=U_f[v].unsqueeze(1).to_broadcast([128, 4, 128]),
                 in1=da_tok[:, hs].unsqueeze(2).to_broadcast([128, 4, 128]), op=ALU.mult), reads=[R_datok, R_const], writes=[R_rc[rb]])
            P.op("pool", lambda e: e.tensor_tensor(out=rhs2[rb][:], in0=MB_f[v].unsqueeze(1).to_broadcast([128, 4, 128]),
                 in1=nacum[:, hs].unsqueeze(2).to_broadcast([128, 4, 128]), op=ALU.subtract), reads=[R_nacum, R_const], writes=[R_r2[rb]])
            P.op("pe", lambda e: e.matmul(pb(rb), lhsT=ones_f, rhs=rhs_cum[rb][:].rearrange("p a b -> p (a b)"), start=True, stop=True),
                 reads=[R_rc[rb], R_const], writes=[BANK[rb]], inc=True)

        def s1b_pair(ga, gb):
            def eoff(g):
                rb = g % 2
                R_ps = pb(rb).rearrange("p (h f) -> p h f", h=4)
                P.op("act", lambda e: e.activation(out=Eoff[rb][:], in_=R_ps, func=AF.Exp), reads=[BANK[rb]], writes=[R_Eoff[rb]])
                P.op("dve", lambda e: e.tensor_tensor(out=E_t[rb][:], in0=R_ps, in1=rhs2[rb][:], op=ALU.add),
                     reads=[BANK[rb], R_r2[rb]], writes=[R_E[rb]])

            def expe(g):
                rb = g % 2
                P.op("act", lambda e: e.activation(out=E_t[rb][:], in_=E_t[rb][:], func=AF.Exp), reads=[R_E[rb]], writes=[R_E[rb]])
            eoff(ga)
            eoff(gb)
            expe(ga)
            expe(gb)

        def stage2(g):
            rb = g % 2
            hs = slice(4 * g, 4 * g + 4)
            P.op("dve", lambda e: e.tensor_tensor(out=MT[rb][:], in0=E_t[rb][:], in1=cbm[:, g, :].unsqueeze(1).to_broadcast([128, 4, 128]), op=ALU.mult),
                 reads=[R_E[rb], R_cbm], writes=[R_MT[rb]])
            P.op("pool", lambda e: e.tensor_tensor(out=Cp_all[:, hs, :], in0=Eoff[rb][:],
                 in1=xbc_c[:, 12 + g, cols].unsqueeze(1).to_broadcast([128, 4, 128]), op=ALU.mult),
                 reads=[R_Eoff[rb], R_xc[12 + g]], writes=[R_Cp[g]])
            if not samp:
                P.op("dve", lambda e: e.tensor_copy(out=cd[:, hs], in_=Eoff[rb][:, :, 127]), reads=[R_Eoff[rb]], writes=[R_cd])
                P.op("dve", lambda e: e.tensor_tensor(out=xdtd[:, hs, :], in0=xdt_tok[:, hs, :],
                     in1=E_t[rb][:, :, 127].unsqueeze(2).to_broadcast([128, 4, 64]), op=ALU.mult), reads=[R_E[rb], R_xdt], writes=[R_xdtd])
            else:
                P.op("dve", lambda e: e.tensor_tensor(out=E_t[rb][:], in0=E_t[rb][:], in1=sel_f.unsqueeze(1).to_broadcast([128, 4, 128]), op=ALU.mult),
                     reads=[R_E[rb], R_const, R_MT[rb]], writes=[R_E[rb]])
                P.op("dve", lambda e: e.tensor_reduce(out=cd[:, hs], in_=E_t[rb][:], axis=mybir.AxisListType.X, op=ALU.add),
                     reads=[R_E[rb]], writes=[R_cd])
                P.op("dve", lambda e: e.tensor_tensor(out=xdtd[:, hs, :], in0=xdt_tok[:, hs, :],
                     in1=cd[:, hs].unsqueeze(2).to_broadcast([128, 4, 64]), op=ALU.mult), reads=[R_cd, R_xdt], writes=[R_xdtd])
            for hh in range(4):
                h = 4 * g + hh
                j, half = h // 2, h % 2
                yb = 2 + (j // 4)
                osl = y_ps[64 * half:64 * half + 64, j, :]
                last = (not has_state) and (not samp)
                if half == 0:
                    P.op("pe", lambda e, j=j: e.matmul(y_ps[:, j, :], lhsT=diagD[:, j, :], rhs=xbc_c[:, j, cols], start=(not samp), stop=False, skip_group_check=samp),
                         reads=[R_diagD, R_xc[j]], writes=[BANK[yb]], inc=False)
                P.op("pe", lambda e, h=h, hh=hh, osl=osl, last=last: e.matmul(osl, lhsT=xdt_tok[:, h, :], rhs=MT[rb][:, hh, :], start=False, stop=last, skip_group_check=samp),
                     reads=[R_xdt, R_MT[rb]], writes=[BANK[yb]], inc=(hh == 3))
                if has_state:
                    P.op("pe", lambda e, h=h, hh=hh, osl=osl, half=half: e.matmul(osl, lhsT=sT_bf[:, h, :], rhs=Cp_all[:, h, :], start=False, stop=True),
                         reads=[R_sTbf[g], R_Cp[g]], writes=[BANK[yb]], inc=(hh == 3))
            if not samp:
                S_ps = pb(7, 256, 256 * (g % 2))
                P.op("pe", lambda e, S_ps=S_ps: e.matmul(S_ps, lhsT=B_tok[:, g, :], rhs=xdtd[:, hs, :].rearrange("p a b -> p (a b)"),
                     start=True, stop=True), reads=[R_Btok, R_xdtd], writes=[BANK[7]], inc=True)

        def stage2b(g):
            hs = slice(4 * g, 4 * g + 4)
            if not samp:
                S_ps = pb(7, 256, 256 * (g % 2))
                sTg = sT[:, hs, :]
                S3 = S_ps.rearrange("p (a b) -> p a b", a=4)
                if not has_state:
                    P.op("act", lambda e: e.copy(out=sTg, in_=S3), reads=[BANK[7]], writes=[R_sT[g]])
                else:
                    P.op("dve", lambda e: e.tensor_tensor(out=tmpS[:], in0=sTg, in1=cd[:, hs].unsqueeze(2).to_broadcast([128, 4, 64]), op=ALU.mult),
                         reads=[R_sT[g], R_cd], writes=[R_tmpS])
                    P.op("dve", lambda e: e.tensor_tensor(out=sTg, in0=tmpS[:], in1=S3, op=ALU.add),
                         reads=[R_tmpS, BANK[7]], writes=[R_sT[g]])
                P.op("act", lambda e: e.copy(out=sT_bf[:, hs, :], in_=sTg), reads=[R_sT[g]], writes=[R_sTbf[g]])

        s1a(0)
        s1a(1)
        s1b_pair(0, 1)
        s1a(2)
        s1a(3)
        if next_prologue is not None:
            next_prologue()
        stage2(0)
        stage2(1)
        s1b_pair(2, 3)
        stage2b(0)
        stage2b(1)
        stage2(2)
        stage2(3)
        stage2b(2)
        stage2b(3)
        if samp:
            ssd_sample_state(l, y_ps)
        P.op("dve", lambda e: e.tensor_tensor(out=yg[:], in0=y_ps, in1=sz[:, :, cols], op=ALU.mult), reads=[BANK[2], BANK[3], R_sz], writes=[R_yg])
        P.op("act", lambda e: e.activation(out=y_sq[:], in_=yg[:], func=AF.Square), reads=[R_yg], writes=[R_ysq])
        gn_ps = pb(6).rearrange("p (g f) -> p g f", g=4)
        for g in range(4):
            for q in range(2):
                P.op("pe", lambda e, g=g, q=q: e.matmul(gn_ps[:, g, :], lhsT=ones_b[:], rhs=y_sq[:, 2 * g + q, :], start=(q == 0), stop=(q == 1)),
                     reads=[R_ysq, R_cb], writes=[BANK[6]], inc=(g == 3 and q == 1))
        P.op("act", lambda e: e.activation(out=g_rstd[:].rearrange("p a b -> p (a b)"), in_=pb(6), func=AF.Ln, bias=EPS, scale=1.0 / 256),
             reads=[BANK[6]], writes=[R_grstd])
        P.op("act", lambda e: e.activation(out=g_rstd[:].rearrange("p a b -> p (a b)"), in_=g_rstd[:].rearrange("p a b -> p (a b)"), func=AF.Exp, scale=-0.5),
             reads=[R_grstd], writes=[R_grstd])
        P.op("dve", lambda e: e.tensor_tensor(out=yg[:], in0=yg[:], in1=vecsT[:, :, V_SSDN(l):V_SSDN(l) + 1].to_broadcast([128, 8, 128]), op=ALU.mult),
             reads=[R_yg, R_tab], writes=[R_yg])

        def evac_tail():
            P.op("dve", lambda e: e.tensor_tensor(out=ynT[:, :, cols].rearrange("p (g q) c -> p g q c", q=2), in0=yg[:].rearrange("p (g q) c -> p g q c", q=2),
                 in1=g_rstd[:].unsqueeze(2).to_broadcast([128, 4, 2, 128]), op=ALU.mult), reads=[R_yg, R_grstd], writes=[R_yn])
        pend_evac.append(evac_tail)
        if (not samp) and gchunk == LP // 128 - 1:
            pend_evac.pop()()
            tr_ps = psum[:, 0:1024].rearrange("p (j f) -> p j f", j=8)
            for j in range(8):
                P.op("pe", lambda e, j=j: e.matmul(tr_ps[:, j, :], lhsT=sT[:, 2 * j:2 * j + 2, :].rearrange("p a b -> p (a b)"), rhs=ident_f, start=True, stop=True),
                     reads=[R_sT[j // 2], R_const], writes=[BANK[j // 4]], inc=(j % 4 == 3))
            P.op("act", lambda e: e.copy(out=yg[:], in_=tr_ps), reads=[BANK[0], BANK[1]], writes=[R_yg])
            P.op("sp", lambda e: e.dma_start(out=ssm_p_out[l].rearrange("(j p) n -> p j n", p=128), in_=yg[:]),
                 reads=[R_yg], writes=[R_out], dma_sem=osem["yg"])

    def ssd_sample_state(l, y_ps):
        P.op("dve", lambda e: e.tensor_copy(out=da_exp[:], in_=da_tok[:].unsqueeze(2).to_broadcast([128, 16, 64])), reads=[R_datok], writes=[R_daexp, R_rc[0], R_r2[0]])
        cdA_ps = pb(5, 128, 320).rearrange("p (j b) -> p j b", j=8)
        for j in range(8):
            P.op("pe", lambda e, j=j: e.matmul(cdA_ps[:, j, :], lhsT=da_exp[:, 2 * j:2 * j + 2, :].rearrange("p a b -> p (a b)"), rhs=ind_f, start=True, stop=True),
                 reads=[R_daexp, R_rc[0], R_r2[0], R_const], writes=[BANK[5]], inc=(j == 7))
        P.op("act", lambda e: e.activation(out=cdA[:], in_=cdA_ps, func=AF.Exp), reads=[BANK[5]], writes=[R_cdA])
        for b in range(NS):
            q = b % 2
            P.op("pool", lambda e, b=b, q=q: e.dma_start(out=s0T_bf[q][:].rearrange("p a b -> p (a b)"), in_=s0T_in[l, b]), writes=[R_s0T[q]], dma_sem=s0T_sem[q])
            P.op("sp", lambda e, b=b, q=q: e.dma_start(out=s0A[q][:], in_=s0A_in[l, b].rearrange("(j p) n -> p j n", p=128)), writes=[R_s0A[q]], dma_sem=s0A_sem[q])
            for h in range(16):
                j, half = h // 2, h % 2
                yb = 2 + (j // 4)
                P.op("pe", lambda e, h=h, j=j, half=half, b=b, q=q: e.matmul(y_ps[64 * half:64 * half + 64, j, 8 * b:8 * b + 8], lhsT=s0T_bf[q][:, h, :],
                     rhs=Cp_all[:, h, 8 * b:8 * b + 8], start=False, stop=(b == NS - 1), skip_group_check=True),
                     reads=[R_s0T[q], R_Cp[h // 4]], writes=[BANK[yb]], inc=(h == 15))
            P.op("dve", lambda e, b=b, q=q: e.tensor_scalar(out=Bm[q][:], in0=B_tok[:], scalar1=ind_f[:, b:b + 1], scalar2=None, op0=ALU.mult),
                 reads=[R_Btok, R_const], writes=[R_Bm[q]])
            sn_ps = psum[:, 0:1024].rearrange("p (j f) -> p j f", j=8)
            for j in range(8):
                P.op("pe", lambda e, j=j, q=q: e.matmul(sn_ps[:, j, :], lhsT=xdtd[:, 2 * j:2 * j + 2, :].rearrange("p a b -> p (a b)"), rhs=Bm[q][:, j // 2, :], start=True, stop=True),
                     reads=[R_xdtd, R_Bm[q]], writes=[BANK[j // 4]], inc=(j % 4 == 3))
            P.op("dve", lambda e, b=b, q=q: e.tensor_tensor(out=s0A[q][:], in0=s0A[q][:], in1=cdA[:, :, b:b + 1].to_broadcast([128, 8, 128]), op=ALU.mult),
                 reads=[R_s0A[q], R_cdA], writes=[R_s0A[q]])
            P.op("dve", lambda e, q=q: e.tensor_tensor(out=s0A[q][:], in0=s0A[q][:], in1=sn_ps, op=ALU.add), reads=[R_s0A[q], BANK[0], BANK[1]], writes=[R_s0A[q]])
            P.op("sp", lambda e, b=b, q=q: e.dma_start(out=ssm_s_out[l, b].rearrange("(j p) n -> p j n", p=128), in_=s0A[q][:]),
                 reads=[R_s0A[q]], writes=[R_out], dma_sem=osem["s0A%d" % q])

    def final_stage():
        for k in range(5):
            final_tile(k)

    def final_tile(k):
        t0, nt, samp = TILES[k]
        if True:
            b = 7
            for kc in range(8):
                s = kc % 2
                P.op("act", lambda e, kc=kc, s=s: e.activation(out=n_sq[s][:, 0:nt], in_=xT[:, kc, t0:t0 + nt], func=AF.Square),
                     reads=[XR[k]], writes=[R_nsq[s]])
                P.op("pe", lambda e, kc=kc, s=s: e.matmul(pb(b, nt), lhsT=ones_b[:], rhs=n_sq[s][:, 0:nt], start=(kc == 0), stop=(kc == 7)),
                     reads=[R_nsq[s], R_cb], writes=[BANK[b]], inc=True)
            rstd_from_psum(b, nt, 1.0 / D, n_rstd[:, 0:nt], n_rstd[:, 0:nt], R_nrstd, R_nrstd)
            for kc in range(8):
                P.op("dve", lambda e, kc=kc: e.scalar_tensor_tensor(out=xT[:, kc, t0:t0 + nt], in0=xT[:, kc, t0:t0 + nt],
                     scalar=vecsT[:, kc, V_NFIN:V_NFIN + 1], in1=n_rstd[:, 0:nt], op0=ALU.mult, op1=ALU.mult),
                     reads=[XR[k], R_tab, R_nrstd], writes=[XR[k]])
            dump_x(k)

    ysrc = yT_out.rearrange("(kc p) t -> p kc t", p=128)

    def dump_x(k):
        t0, nt, _ = TILES[k]
        P.op("sp", lambda e: e.dma_start(out=ysrc[:, :, t0:t0 + nt], in_=xT[:, :, t0:t0 + nt]), reads=[XR[k]], writes=[R_out], dma_sem=osem["x"])

    stage = [0]

    def done():
        stage[0] += 1
        return stop_after is not None and stage[0] >= stop_after

    def run_all():
        for l in range(2):
            mod_ensure(l, 2)
            for st in SUPER:
                ffn_stage(l, 0, st)
            if done():
                return False
            mod_ensure(l, 5)
            handoff(FFN_RES, MIX_RES)
            for k in range(5):
                mix_stage(l, k)
            if done():
                return False
            handoff(MIX_RES, FFN_RES)
            mod_ensure(l, 8)
            for st in SUPER:
                ffn_stage(l, 1, st)
            if done():
                return False
        return True

    if run_all():
        final_stage()
    else:
        for k in range(5):
            dump_x(k)
    for sm in osem.values():
        if sm.val > 0:
            P.q["sp"].append(("w", sm, sm.val))

    with nc.Block() as block:
        @block.tensor
        def _(e):
            P.replay("pe", e)

        @block.scalar
        def _(e):
            P.replay("act", e)

        @block.vector
        def _(e):
            P.replay("dve", e)

        @block.gpsimd
        def _(e):
            P.replay("pool", e)

        @block.sync
        def _(e):
            P.replay("sp", e)
    stack.close()
    return nc, P


def make_consts():
    c = np.zeros((128, 10, 128), np.float32)
    s = np.arange(128)[:, None]
    l = np.arange(128)[None, :]
    c[:, 0, :] = np.eye(128, dtype=np.float32)
    c[:, 1, :] = 1.0
    NEG = -30000.0
    c[:, 2, :] = (s <= l)
    c[:, 3, :] = np.where(l >= s, 0.0, NEG)
    c[:, 4, :] = (l >= s)
    same = (s // LS) == (l // LS)
    c[:, 5, :] = same & (s <= l)
    c[:, 6, :] = np.where(same & (l >= s), 0.0, NEG)
    c[:, 7, :] = same & (l >= s)
    c[:, 8, 0:16] = (s // LS) == np.arange(16)[None, :]
    c[:, 9, :] = (l == (s // LS) * LS + LS - 1)
    return c


def prep_inputs(inp):
    f = lambda a: np.ascontiguousarray(np.asarray(a, dtype=np.float32))
    shared = {}
    def blk(w):
        w = np.asarray(w, dtype=np.float32)
        n = w.shape[2]
        return np.ascontiguousarray(w.reshape(2, 8, 128, n // WB, WB).transpose(0, 3, 2, 1, 4)).reshape(2, n // WB, 128, 8 * WB)
    for n in ["w_ada", "ffn1_w_gu", "ffn2_w_gu", "w_out_ssd", "w_out_sc", "w_o"]:
        shared[n] = blk(inp[n])
    win = np.asarray(inp["w_in"], dtype=np.float32)
    shared["w_in"] = blk(np.concatenate([win[:, :, :C_DT_RAW], win[:, :, C_DT_RAW + 16:]], axis=2))
    shared["w_dt"] = f(win[:, :, C_DT_RAW:C_DT_RAW + 16].reshape(2, 8, 128, 16).transpose(0, 2, 1, 3).reshape(2, 128, 128))
    for n in ["ffn1_w_down", "ffn2_w_down"]:
        w = np.asarray(inp[n], dtype=np.float32)
        shared[n] = np.ascontiguousarray(w.reshape(2, NJ, 128, 8, 128).transpose(0, 3, 2, 1, 4)).reshape(2, 8, 128, NJ * 128)
    shared["badaT_in"] = f(np.asarray(inp["b_ada"]).reshape(2, 72, 128).transpose(2, 0, 1))
    vecs = np.zeros((NV, D), np.float32)
    for l in range(2):
        vecs[V_NF1(l)] = inp["norm_ffn1"][l]
        vecs[V_NMIX(l)] = inp["norm_mix"][l]
        vecs[V_NF2(l)] = inp["norm_ffn2"][l]
        vecs[V_SSDN(l)] = inp["ssd_norm"][l]
        vecs[V_DP(l)] = np.repeat(np.asarray(inp["ssd_d"][l]), 64)
        for k in range(3):
            vecs[V_SCW(l, k)] = inp["sc_conv_w"][l][k]
    vecs[V_NFIN] = inp["norm_final"]
    shared["vecsT_in"] = f(vecs.reshape(NV, 8, 128).transpose(2, 1, 0))
    xv = np.zeros((NX, 2048), np.float32)
    for l in range(2):
        for k in range(4):
            xv[X_CW(l, k)] = inp["ssd_conv_w"][l][k]
        xv[X_CB(l)] = inp["ssd_conv_b"][l]
    shared["xbcv_in"] = f(xv.reshape(NX, 16, 128).transpose(2, 1, 0))
    shared["dtb_in"] = f(np.asarray(inp["ssd_dt_bias"]).T)
    shared["alog_in"] = f(np.broadcast_to(np.asarray(inp["ssd_a_log"]).reshape(1, 32), (128, 32)))
    shared["consts_in"] = make_consts()
    maps = []
    for i in range(NCORES):
        m = dict(shared)
        xs = np.asarray(inp["x_sample"][NS * i:NS * (i + 1)]).reshape(NS * LS, D)
        m["xT_in"] = f(np.concatenate([np.asarray(inp["x_prompt"][i]), xs], axis=0).T)
        call = np.concatenate([np.asarray(inp["c_prompt"][i:i + 1]), np.asarray(inp["c_sample"][NS * i:NS * (i + 1)])], axis=0)
        m["cT_in"] = f(call.reshape(17, 8, 128).transpose(2, 1, 0))
        st = np.asarray(inp["state_ssm"][:, NS * i:NS * (i + 1)]).reshape(2, NS, 1024, 128)
        m["s0A_in"] = f(st)
        m["s0T_in"] = f(st.transpose(0, 1, 3, 2))
        cx = np.asarray(inp["state_conv_ssd"][:, NS * i:NS * (i + 1)])
        m["csx_in"] = f(cx.reshape(2, NS, 3, 16, 128).transpose(0, 4, 3, 1, 2))
        cs = np.asarray(inp["state_conv_short"][:, NS * i:NS * (i + 1)])
        m["css_in"] = f(cs.reshape(2, NS, 2, 8, 128).transpose(0, 4, 3, 1, 2))
        maps.append(m)
    return maps


_CACHE = {}


def kernel(**inputs):
    if "nc" not in _CACHE:
        _CACHE["nc"] = build_program()[0]
    nc = _CACHE["nc"]
    maps = prep_inputs(inputs)
    res = run_bass_kernel_spmd(nc, maps, core_ids=list(range(NCORES)))
    R = res.results
    y_p = np.stack([R[i]["yT_out"][:, :LP].T for i in range(NCORES)], axis=0)
    y_s = np.concatenate([R[i]["yT_out"][:, LP:].T.reshape(NS, LS, D) for i in range(NCORES)], axis=0)
    ssm_p = np.stack([R[i]["ssm_p_out"].reshape(2, 16, 64, 128) for i in range(NCORES)], axis=1)
    csx_p = np.stack([R[i]["csx_p_out"] for i in range(NCORES)], axis=1)
    css_p = np.stack([R[i]["css_p_out"] for i in range(NCORES)], axis=1)
    ssm_s = np.concatenate([R[i]["ssm_s_out"].reshape(2, NS, 16, 64, 128) for i in range(NCORES)], axis=1)
    csx_s = np.concatenate([R[i]["csx_s_out"] for i in range(NCORES)], axis=1)
    css_s = np.concatenate([R[i]["css_s_out"] for i in range(NCORES)], axis=1)
    outs = (y_p, y_s, ssm_p, csx_p, css_p, ssm_s, csx_s, css_s)
    return tuple(np.ascontiguousarray(o, dtype=np.float32) for o in outs)
```
